# Optimizing a Trainium2 kernel written in Bass

```python
import math, functools
import jax, jax.numpy as jnp
from jax import lax
import numpy as np

D_MODEL = 1024
BATCH = 16
SEQ = 4096
DEPTH = 1
DEC_BATCH = 128
DEC_SEQ = 8
PAST_LEN = 8192
PAGE_SIZE = 128

D_MIX = D_MODEL
HD_A = 64
H_A = (D_MIX // 2) // HD_A
D_A = H_A * HD_A
H_B = 4
DV_B = (D_MIX - D_A) // H_B
DK_B = DV_B // 2
D_B = H_B * DV_B
GATE_RANK = 16
GATE_TAU = 16.0
GLA_CHUNK = 64
DILATED = ((128, 1), (512, 4), (2048, 16))
MAX_WINDOW = 2048
BAND_BLOCK = 128
ROPE_THETA = 10000.0
EPS = 1e-6
SPLIT_SIZES = (D_A, D_A, D_A, H_B * DK_B, H_B * DK_B, D_B, GATE_RANK, D_A, D_B)
D_IN = sum(SPLIT_SIZES)

kernel_name = 'hymba_dilated_swa_gla_step'


def split_points():
    return [int(i) for i in np.cumsum(SPLIT_SIZES)[:-1]]


def rmsnorm(x, g):
    xf = x.astype(jnp.float32)
    y = xf * lax.rsqrt(jnp.mean(xf * xf, axis=-1, keepdims=True) + EPS)
    return (y * g.astype(jnp.float32)).astype(x.dtype)


def rope(x, pos):
    half = x.shape[-1] // 2
    inv_freq = ROPE_THETA ** (-jnp.arange(half, dtype=jnp.float32) / half)
    ang = pos.astype(jnp.float32)[:, None] * inv_freq[None, :]
    cos = jnp.cos(ang)[None, :, None, :]
    sin = jnp.sin(ang)[None, :, None, :]
    xf = x.astype(jnp.float32)
    x1, x2 = xf[..., :half], xf[..., half:]
    return jnp.concatenate([x1 * cos - x2 * sin, x2 * cos + x1 * sin], axis=-1).astype(x.dtype)


def softmax_attend(s, v, out_spec):
    m = jnp.max(s, axis=-1, keepdims=True)
    p = jnp.exp(s - m)
    den = jnp.sum(p, axis=-1)
    o = jnp.einsum(out_spec, p, v) / den[..., None]
    return o, m[..., 0] + jnp.log(den)


def dilated_branch_prompt(q, k, v, window, dil):
    B, S, H, hd = q.shape
    n_keys = window // dil + 1
    lm = S // dil
    nb = -(-lm // BAND_BLOCK)
    lp = nb * BAND_BLOCK

    def by_residue(a):
        a = a.reshape(B, lm, dil, H, hd).transpose(0, 2, 1, 3, 4)
        a = jnp.pad(a, ((0, 0), (0, 0), (0, lp - lm), (0, 0), (0, 0)))
        return a.reshape(B, dil, nb, BAND_BLOCK, H, hd)

    def with_prev(a):
        prev = jnp.concatenate([jnp.zeros_like(a[:, :, :1]), a[:, :, :-1]], axis=2)
        return jnp.concatenate([prev, a], axis=3)

    qr = by_residue(q)
    kk = with_prev(by_residue(k))
    vv = with_prev(by_residue(v))
    qi = jnp.arange(BAND_BLOCK)[:, None] + BAND_BLOCK
    ki = jnp.arange(2 * BAND_BLOCK)[None, :]
    dist = qi - ki
    key_m = (jnp.arange(nb)[:, None, None] - 1) * BAND_BLOCK + ki[None]
    valid = (dist >= 0) & (dist < n_keys) & (key_m >= 0)
    s = jnp.einsum('brnqhd,brnkhd->brnqhk', qr, kk) * (hd ** -0.5)
    s = jnp.where(valid[None, None, :, :, None, :], s, -jnp.inf)
    o, lse = softmax_attend(s, vv, 'brnqhk,brnkhd->brnqhd')
    o = o.reshape(B, dil, lp, H, hd)[:, :, :lm].transpose(0, 2, 1, 3, 4).reshape(B, S, H, hd)
    lse = lse.reshape(B, dil, lp, H)[:, :, :lm].transpose(0, 2, 1, 3).reshape(B, S, H)
    return o, lse


def dilated_branch_sample(q, k_buf, v_buf, window, dil):
    T = q.shape[1]
    W = k_buf.shape[1] - T
    hd = q.shape[-1]
    n_keys = window // dil + 1
    idx = W + jnp.arange(T)[:, None] - dil * jnp.arange(n_keys)[None, :]
    valid = idx >= 0
    idx = jnp.maximum(idx, 0)
    kg = k_buf[:, idx]
    vg = v_buf[:, idx]
    s = jnp.einsum('bthd,btnhd->bthn', q, kg) * (hd ** -0.5)
    s = jnp.where(valid[None, :, None, :], s, -jnp.inf)
    return softmax_attend(s, vg, 'bthn,btnhd->bthd')


def dilated_mixture(results):
    outs = jnp.stack([r[0] for r in results], axis=0)
    lses = jnp.stack([r[1] for r in results], axis=0)
    alpha = jax.nn.softmax(lses, axis=0)
    return jnp.einsum('gblh,gblhd->blhd', alpha, outs)


def attend_prompt(q, k, v):
    qf, kf, vf = (a.astype(jnp.float32) for a in (q, k, v))
    return dilated_mixture([dilated_branch_prompt(qf, kf, vf, w, d) for w, d in DILATED])


def attend_sample(k_past, v_past, q, k, v):
    qf = q.astype(jnp.float32)
    k_buf = jnp.concatenate([k_past.astype(jnp.float32), k.astype(jnp.float32)], axis=1)
    v_buf = jnp.concatenate([v_past.astype(jnp.float32), v.astype(jnp.float32)], axis=1)
    return dilated_mixture([dilated_branch_sample(qf, k_buf, v_buf, w, d) for w, d in DILATED])


def gla_chunked(q, k, v, log_a, s0):
    B, L, H, dk = q.shape
    dv = v.shape[-1]
    c = math.gcd(L, GLA_CHUNK)
    n = L // c

    def to_chunks(a):
        return a.astype(jnp.float32).reshape(B, n, c, H, a.shape[-1]).transpose(1, 0, 3, 2, 4)

    causal = jnp.tril(jnp.ones((c, c), dtype=bool))

    def step(state, xs):
        qc, kc, vc, gc = xs
        b = jnp.cumsum(gc, axis=2)
        diff = b[:, :, :, None, :] - b[:, :, None, :, :]
        decay = jnp.exp(jnp.where(causal[:, :, None], diff, -jnp.inf))
        scores = jnp.einsum('bhtk,bhsk,bhtsk->bhts', qc, kc, decay)
        o = (jnp.einsum('bhts,bhsv->bhtv', scores, vc)
             + jnp.einsum('bhtk,bhkv->bhtv', qc * jnp.exp(b), state))
        b_last = b[:, :, -1]
        state = (jnp.exp(b_last)[..., None] * state
                 + jnp.einsum('bhsk,bhsv->bhkv', kc * jnp.exp(b_last[:, :, None] - b), vc))
        return state, o

    s_fin, o = lax.scan(step, s0.astype(jnp.float32),
                        (to_chunks(q), to_chunks(k), to_chunks(v), to_chunks(log_a)))
    return o.transpose(1, 0, 3, 2, 4).reshape(B, L, H, dv), s_fin


def mixer_layer(x, pos, attend, s0, norm_w, w_in, w_gate_up, b_gate, q_norm_w, k_norm_w,
                gla_norm_w, w_out):
    B, L, _ = x.shape
    h = rmsnorm(x, norm_w)
    z = h @ w_in
    q, k, v, qb, kb, vb, g_lr, gate_a, gate_b = jnp.split(z, split_points(), axis=-1)
    heads = lambda a, nh: a.reshape(B, L, nh, -1)
    q = rope(rmsnorm(heads(q, H_A), q_norm_w), pos)
    k = rope(rmsnorm(heads(k, H_A), k_norm_w), pos)
    v = heads(v, H_A)
    o_a = attend(q, k, v)
    log_a = jax.nn.log_sigmoid((g_lr @ w_gate_up + b_gate).astype(jnp.float32)) / GATE_TAU
    o_b, s_fin = gla_chunked(heads(qb, H_B) * (DK_B ** -0.5), heads(kb, H_B), heads(vb, H_B),
                             heads(log_a, H_B), s0)
    o_b = rmsnorm(o_b, gla_norm_w)
    mixed = jnp.concatenate(
        [o_a.reshape(B, L, D_A) * jax.nn.silu(gate_a.astype(jnp.float32)),
         o_b.reshape(B, L, D_B) * jax.nn.silu(gate_b.astype(jnp.float32))], axis=-1)
    y = x + mixed.astype(x.dtype) @ w_out
    return y, k, v, s_fin


def setup_inputs(seed: int = 0) -> dict:
    key = jax.random.key(seed)
    ks = jax.random.split(key, 13)
    win = min(MAX_WINDOW, PAST_LEN)
    nrm = lambda kk, shape, scale: scale * jax.random.normal(kk, shape, dtype=jnp.float32)
    return {
        'x_prompt': nrm(ks[0], (BATCH, SEQ, D_MODEL), 1.0),
        'x_sample': nrm(ks[1], (DEC_BATCH, DEC_SEQ, D_MODEL), 1.0),
        'cache_k_win': nrm(ks[2], (DEPTH, DEC_BATCH, win, H_A, HD_A), 1.0),
        'cache_v_win': nrm(ks[3], (DEPTH, DEC_BATCH, win, H_A, HD_A), 1.0),
        'state_gla': nrm(ks[4], (DEPTH, DEC_BATCH, H_B, DK_B, DV_B), 1.0),
        'norm_w': 1.0 + nrm(ks[5], (DEPTH, D_MODEL), 0.02),
        'w_in': nrm(ks[6], (DEPTH, D_MODEL, D_IN), D_MODEL ** -0.5),
        'w_gate_up': nrm(ks[7], (DEPTH, GATE_RANK, H_B * DK_B), GATE_RANK ** -0.5),
        'b_gate': nrm(ks[8], (DEPTH, H_B * DK_B), 0.1),
        'q_norm_w': 1.0 + nrm(ks[9], (DEPTH, HD_A), 0.02),
        'k_norm_w': 1.0 + nrm(ks[10], (DEPTH, HD_A), 0.02),
        'gla_norm_w': 1.0 + nrm(ks[11], (DEPTH, DV_B), 0.02),
        'w_out': nrm(ks[12], (DEPTH, D_MIX, D_MODEL), D_MIX ** -0.5),
    }


def reference(x_prompt, x_sample, cache_k_win, cache_v_win, state_gla, norm_w, w_in, w_gate_up,
              b_gate, q_norm_w, k_norm_w, gla_norm_w, w_out):
    bp, sp, _ = x_prompt.shape
    bs, ts, _ = x_sample.shape
    pos_p = jnp.arange(sp)
    pos_s = PAST_LEN + jnp.arange(ts)
    keep_p = min(MAX_WINDOW, sp)
    hp, hs = x_prompt, x_sample
    kp_l, vp_l, sp_l, ks_l, vs_l, ss_l = [], [], [], [], [], []
    for layer in range(DEPTH):
        params = (norm_w[layer], w_in[layer], w_gate_up[layer], b_gate[layer], q_norm_w[layer],
                  k_norm_w[layer], gla_norm_w[layer], w_out[layer])
        s0 = jnp.zeros((bp, H_B, DK_B, DV_B), dtype=jnp.float32)
        hp, kp, vp, st_p = mixer_layer(hp, pos_p, attend_prompt, s0, *params)
        att_s = functools.partial(attend_sample, cache_k_win[layer], cache_v_win[layer])
        hs, kn, vn, st_s = mixer_layer(hs, pos_s, att_s, state_gla[layer], *params)
        kp_l.append(kp[:, sp - keep_p:])
        vp_l.append(vp[:, sp - keep_p:])
        sp_l.append(st_p.astype(state_gla.dtype))
        ks_l.append(kn)
        vs_l.append(vn)
        ss_l.append(st_s.astype(state_gla.dtype))
    return (hp, hs, jnp.stack(kp_l), jnp.stack(vp_l), jnp.stack(sp_l),
            jnp.stack(ks_l), jnp.stack(vs_l), jnp.stack(ss_l))
```

```python
import numpy as np
from contextlib import ExitStack
import ml_dtypes
import concourse.bass as bass
import concourse.mybir as mybir
from concourse.bass_utils import run_bass_kernel_spmd
from concourse.alu_op_type import AluOpType as ALU

F32 = mybir.dt.float32
BF16 = mybir.dt.bfloat16
AF = mybir.ActivationFunctionType
AX = mybir.AxisListType
NPBF = ml_dtypes.bfloat16
RING = 18
KSTOP = 0
KATT = 9
ARATIO = 0
NA_EST = 60
A0RATIO = 1
A1START = 10
A0HEAD = 1
SKIPG = 0
EVM = 2
A2START = 999
A2RATE = 1
MPOOL = 0
EPS = 1e-6


class Buf:
    def __init__(self, name):
        self.name = name
        self.w = None
        self.r = []


class Bufs:
    def __init__(self):
        self.d = {}

    def __getattr__(self, k):
        d = self.__dict__["d"]
        if k not in d:
            d[k] = Buf(k)
        return d[k]

    def get(self, k):
        return getattr(self, k)


class Sched:
    def __init__(self, nc, stack):
        self.nc = nc
        self.stack = stack
        self.eng = {"pe": nc.tensor, "act": nc.scalar, "dve": nc.vector, "pool": nc.gpsimd, "sp": nc.sync}
        self.sems = {}
        self.cnt = {}
        self.known = {e: {} for e in self.eng}
        for e in self.eng:
            self._sem("E_" + e)

    def _sem(self, key):
        if key not in self.sems:
            self.sems[key] = self.stack.enter_context(self.nc.semaphore(key))
            self.cnt[key] = 0
        return self.sems[key]

    def _deps(self, e, reads, writes, acc=()):
        deps = {}

        def add(d):
            if d is None:
                return
            k, v = d
            if e == "pe" and k == "E_pe":
                return
            if deps.get(k, 0) < v:
                deps[k] = v
        for b in reads:
            add(b.w)
        for b in writes:
            add(b.w)
            for r in b.r:
                add(r)
        for b in acc:
            add(b.w)
            for r in b.r:
                add(r)
        for k, v in deps.items():
            if self.known[e].get(k, 0) < v:
                self.eng[e].wait_ge(self.sems[k], v)
                self.known[e][k] = v

    def op(self, e, fn, r=(), w=(), a=()):
        self._deps(e, r, w, a)
        inst = fn(self.eng[e])
        key = "E_" + e
        self.cnt[key] += 1
        inst.then_inc(self.sems[key], 1)
        tok = (key, self.cnt[key])
        for b in w:
            b.w = tok
            b.r = []
        for b in a:
            b.w = tok
        for b in r:
            b.r.append(tok)
        return tok

    def dma(self, e, key, out, in_, r=(), w=()):
        self._deps(e, r, w)
        sem = self._sem("D_" + key)
        inst = self.eng[e].dma_start(out=out, in_=in_)
        self.cnt["D_" + key] += 16
        inst.then_inc(sem, 16)
        tok = ("D_" + key, self.cnt["D_" + key])
        for b in w:
            b.w = tok
            b.r = []
        for b in r:
            b.r.append(tok)
        return tok

    def barrier(self):
        for e in self.eng:
            self.wait_all(e)

    def wait_all(self, e):
        for k, v in self.cnt.items():
            if v > 0 and self.known[e].get(k, 0) < v:
                self.eng[e].wait_ge(self.sems[k], v)
                self.known[e][k] = v


def make_consts(NT):
    c = {}
    c["ident"] = np.eye(128, dtype=np.float32)
    c["identb"] = np.eye(128, dtype=np.float32).astype(NPBF)
    k = np.arange(128)[:, None, None]
    d = np.arange(17)[None, :, None]
    q = np.arange(128)[None, None, :]
    dist = 128 * d + q - k
    m = ((dist >= 0) & (dist <= 128)).astype(np.float32)
    m += ((dist >= 0) & (dist <= 512) & (dist % 4 == 0))
    m += ((dist >= 0) & (dist <= 2048) & (dist % 16 == 0))
    c["mp"] = m.astype(NPBF)
    def cnt(dist):
        m_ = ((dist >= 0) & (dist <= 128)).astype(np.float32)
        m_ += ((dist >= 0) & (dist <= 512) & (dist % 4 == 0))
        m_ += ((dist >= 0) & (dist <= 2048) & (dist % 16 == 0))
        return m_
    p_ = np.arange(128)[:, None, None]
    t = np.arange(8)[None, None, :]
    mm_ = np.arange(6)[None, :, None]
    rows_c = 256 * mm_ + 16 * (p_ % 16) + (p_ // 16)
    m_c = cnt(2048 + t - rows_c)
    cc = np.arange(12, 16)[None, :, None]
    m_t = cnt(2048 + t - (128 * cc + p_))
    s_ = np.arange(16)[None, :, None]
    tk = p_ - 8 * s_
    dist = t - tk
    m_n = ((tk >= 0) & (tk < 8) & (dist >= 0)).astype(np.float32) * (1.0 + (dist % 4 == 0) + (dist % 16 == 0))
    c["ms"] = np.concatenate([m_c, m_t, m_n], axis=1).astype(NPBF)
    a = np.arange(128)
    up = (a[:, None] <= a[None, :]).astype(np.float32)
    same = (a[:, None] // 8 == a[None, :] // 8).astype(np.float32)
    c["gm"] = np.stack([up, np.ones((128, 128), np.float32), up * same, same], axis=1)
    c["gmb"] = np.stack([up, up * same], axis=1).astype(NPBF)
    oh = (a[:, None] // 8 == np.arange(16)[None, :]).astype(np.float32)
    c["oh"] = np.concatenate([np.ones((128, 1), np.float32), oh], axis=1)
    c["ohb"] = oh.astype(NPBF)
    bm = (np.arange(16)[:, None] == (a[None, :] // 8)).astype(np.float32)
    c["bmask"] = np.broadcast_to(bm[None], (64, 16, 128)).astype(NPBF).copy()
    half = 32
    inv = (10000.0 ** (-np.arange(half, dtype=np.float32) / half)).astype(np.float32)
    pos = np.concatenate([np.arange(NT * 128, dtype=np.float32), 8192.0 + (np.arange(128) % 8).astype(np.float32)])
    ang = pos[:, None].astype(np.float32) * inv[None, :]
    cs = np.concatenate([np.cos(ang), np.sin(ang)], axis=1).astype(np.float32)
    c["cs"] = cs.reshape(NT + 1, 128, 64).copy()
    return c


CONST_SPECS = [("ident", [128, 128], F32), ("identb", [128, 128], BF16), ("mp", [128, 17, 128], BF16),
               ("ms", [128, 26, 8], BF16), ("gm", [128, 4, 128], F32), ("gmb", [128, 2, 128], BF16),
               ("oh", [128, 17], F32), ("ohb", [128, 16], BF16), ("bmask", [64, 16, 128], BF16),
               ("nw", [128, 8], F32), ("qknw", [128, 128], F32), ("gnw", [128, 128], F32), ("wgu", [17, 256], F32)]


def build(NSEQ=2, NT=32, SAMPLE=True, NSS=16):
    nc = bass.Bass("TRN2", target_bir_lowering=False)
    KEEP = min(16, NT)
    di = lambda n, s, d=F32: nc.dram_tensor(n, s, d, kind="ExternalInput").ap()
    do = lambda n, s, d=F32: nc.dram_tensor(n, s, d, kind="ExternalOutput").ap()
    xp = di("xp", [NSEQ * NT * 128, 1024])
    xs = di("xs", [128, 1024])
    ck = di("ck", [16, 2048, 512])
    cv = di("cv", [16, 2048, 512])
    sg = di("sg", [16, 4, 64, 128])
    w_in = di("w_in", [1024, 3600])
    w_out = di("w_out", [1024, 1024])
    csd = di("cs", [NT + 1, 128, 64])
    cd = {n: di(n, s, d) for n, s, d in CONST_SPECS}
    yp = do("yp", [NSEQ * NT * 128, 1024])
    ys = do("ys", [128, 1024])
    kwp = do("kwp", [NSEQ, KEEP * 128, 512])
    vwp = do("vwp", [NSEQ, KEEP * 128, 512])
    sgp = do("sgp", [NSEQ, 4, 64, 128])
    kns = do("kns", [128, 512])
    vns = do("vns", [128, 512])
    sgs = do("sgs", [16, 4, 64, 128])

    st = ExitStack()
    with st:
        S = Sched(nc, st)
        B = Bufs()
        sb = lambda n, s, d=F32: st.enter_context(nc.sbuf_tensor(n, s, d))
        wbi = sb("wbi", [128, 8, 3600], BF16)
        wbo = sb("wbo", [128, 8, 1024], BF16)
        kTr = sb("kTr", [128, 4, RING * 128], BF16)
        Vr = sb("Vr", [128, RING, 8, 66], BF16)
        C = {n: sb("c_" + n, s, d) for n, s, d in CONST_SPECS}
        xt = sb("xt", [128, 1024])
        cs = sb("cs_t", [128, 64])
        tmpA = sb("tmpA", [128, 1024])
        tmpS = sb("tmpS", [128, 1024])
        xT = sb("xT", [128, 8, 128], BF16)
        qkraw = sb("qkraw", [128, 1024])
        vf = sb("vf", [128, 512])
        qkb = sb("qkb", [128, 512])
        vbb = sb("vbb", [128, 512], BF16)
        g17 = sb("g17", [128, 17])
        gates_ = [sb("gates%d" % i, [128, 1024]) for i in range(2)]
        qkh = sb("qkh", [128, 1024])
        qT_ = [sb("qT%d" % i, [128, 4, 2, 128], BF16) for i in range(2)]
        mixed_ = [sb("mixed%d" % i, [128, 1024], BF16) for i in range(2)]
        mT = sb("mT", [128, 8, 128], BF16)
        ysb = sb("ysb", [128, 1024])
        smallB = sb("smallB", [128, 16])
        small = sb("small", [128, 64])
        dec16 = sb("dec16", [64, 64])
        g17T = sb("g17T", [17, 128])
        lgl = sb("lgl", [128, 256])
        EB = sb("EB", [128, 256])
        EBi = sb("EBi", [128, 256])
        EBl = sb("EBl", [128, 256])
        qkt = sb("qkt", [128, 768], BF16)
        qkT4 = sb("qkT4", [64, 8, 128], BF16)
        AT = sb("AT", [128, 4, 128], BF16)
        Sst = sb("Sst", [64, 4, 128])
        Sb = sb("Sb", [64, 4, 128], BF16)
        if SAMPLE:
            kst = [sb("kst%d" % i, [128, 512]) for i in range(2)]
            vst0 = sb("vst0", [128, 512])
            vst = [vst0, vst0]
            S0 = sb("S0", [64, 8, 128])
            S0b = sb("S0b", [64, 8, 128], BF16)
            Zq = sb("Zq", [64, 8, 128], BF16)
            VBe = sb("VBe", [128, 8, 128], BF16)
        pTpad = [[sb("pTpad%d%d" % (i, j), [128, 4, 128], BF16) for j in range(2)] for i in range(2)]
        pTs = [pTpad[0][0], pTpad[0][1], pTpad[1][0]]
        pTsB = [B.pTpad00, B.pTpad01, B.pTpad10]
        PB = [st.enter_context(nc.psum_tensor("pb%d" % i, [128, 512], F32)) for i in range(8)]
        PBB = [B.get("pb%d" % i) for i in range(8)]
        P1b = PB[1][:].bitcast(BF16)
        P4b = PB[4][:].bitcast(BF16)
        v4 = lambda ap: ap.rearrange("p (a b) -> p a b", a=4)

        ms = small[:, 0:1]
        rstd = small[:, 1:2]
        ms16 = small[:, 2:18]
        rs16 = small[:, 18:34]
        rden = smallB[:, 0:8]
        ms4 = small[:, 42:46]
        rs4 = small[:, 46:50]
        dec = small[0:64, 50:54]
        ones_c = C["oh"][:, 0:1]
        eps_c = sb("eps_c", [128, 1])

        for n, s, d in CONST_SPECS:
            S.dma("sp", "const", C[n][:], cd[n], w=[B.const])
        S.op("pool", lambda g: g.memset(Vr[:, :, :, 64:66], 1.0), w=[B.get("V%d" % i) for i in range(RING)])
        S.op("pool", lambda g: g.memset(g17[:, 16:17], 1.0), w=[B.g17])
        S.op("pool", lambda g: g.memset(eps_c[:], EPS), w=[B.const2])
        for i in range(2):
            S.op("pool", lambda g: g.memset(qT_[i][:], 0.0), w=[B.get("qT%d" % i)])
        stg_in = [(qkraw, B.qkraw, "qkraw"), (qkh, B.qkh, "qkh"), (tmpA, B.tmpA, "tmpA"), (gates_[0], B.gates0, "gates0"), (gates_[1], B.gates1, "gates1")]
        stg_out = [(tmpS, B.tmpS, "tmpS"), (ysb, B.ysb, "ysb")]
        k_ = 0
        for kc in range(8):
            for (c0, c1) in [(0, 1024), (1024, 2048), (2048, 3072), (3072, 3600)]:
                st_, sb_, sn_ = stg_in[k_ % len(stg_in)]
                k_ += 1
                S.dma("sp", "stage_" + sn_, st_[:, 0:c1 - c0], w_in[kc * 128:(kc + 1) * 128, c0:c1], w=[sb_])
                S.op("dve", lambda v: v.tensor_scalar(out=wbi[:, kc, c0:c1], in0=st_[:, 0:c1 - c0], scalar1=C["nw"][:, kc:kc + 1],
                                                        scalar2=None, op0=ALU.mult), r=[sb_, B.const], w=[B.wbi])
            st_, sb_, sn_ = stg_out[kc % 2]
            S.dma("sp", "stage_" + sn_, st_[:, :], w_out[kc * 128:(kc + 1) * 128, :], w=[sb_])
            S.op("act", lambda a: a.copy(out=wbo[:, kc, :], in_=st_[:, :]), r=[sb_], w=[B.wbo])

        def rsqrt_chain(dst, src, scale, rb, wb):
            S.op("act", lambda a: a.activation(out=dst, in_=src, func=AF.Ln, scale=scale, bias=eps_c[0:dst.shape[0], :]), r=list(rb) + [B.const2], w=wb)
            S.op("act", lambda a: a.activation(out=dst, in_=dst, func=AF.Exp, scale=-0.5), r=wb, w=wb)

        evac_flip = [0]

        def evac(out, in_, rb, wb, scale=None):
            e = "act" if (evac_flip[0] % EVM) != EVM - 1 else "dve"
            evac_flip[0] += 1
            if e == "act":
                if scale is None:
                    S.op("act", lambda a: a.copy(out=out, in_=in_), r=rb, w=wb)
                else:
                    S.op("act", lambda a: a.activation(out=out, in_=in_, func=AF.Copy, scale=scale), r=rb, w=wb)
            else:
                if scale is None:
                    S.op("dve", lambda v: v.tensor_copy(out=out, in_=in_), r=rb, w=wb)
                else:
                    S.op("dve", lambda v: v.tensor_scalar(out=out, in0=in_, scalar1=scale, scalar2=None, op0=ALU.mult), r=rb, w=wb)

        def load_tile(kind, b, T):
            src = xs[:, :] if kind == "s" else xp[(b * NT + T) * 128:(b * NT + T + 1) * 128, :]
            S.dma("sp", "xt", xt[:], src, w=[B.xt])

        def load_cs(kind, b, T):
            S.dma("sp", "cs", cs[:], csd[NT if kind == "s" else T], w=[B.cs])

        def gslot(kind, b, T):
            return (NSEQ * NT if kind == "s" else b * NT + T) % RING

        SLOT_S = gslot("s", 0, 0)
        cslot = lambda c: c if c < SLOT_S else c + 1

        def stageA(kind, b, T, par, nxt):
            samp = kind == "s"
            slot = gslot(kind, b, T)
            gates, qT, mixed = gates_[par], qT_[par], mixed_[par]
            Bg, Bq, Bm = B.get("gates%d" % par), B.get("qT%d" % par), B.get("mixed%d" % par)
            def a0():
                tmpAb = tmpA[:].bitcast(BF16)
                xb16 = tmpAb[:, 1024:2048]
                P0b_ = PB[0][:].bitcast(BF16)
                S.op("dve", lambda v: v.memset(ms, 0.0), w=[B.small])
                yield
                S.op("act", lambda a: a.activation(out=tmpAb[:, 0:1024], in_=xt[:], func=AF.Square, scale=1.0 / 32, accum_out=ms), r=[B.xt], w=[B.tmpA, B.small])
                rsqrt_chain(rstd, ms, 1.0, [B.small], [B.small])
                yield
                evac(xb16, xt[:], [B.xt], [B.tmpA])
                yield
                for kc in range(8):
                    S.op("pe", lambda t: t.transpose(out=P0b_[:, kc * 128:(kc + 1) * 128], in_=xb16[:, kc * 128:(kc + 1) * 128], identity=C["identb"][:]),
                         r=[B.tmpA, B.const], w=[PBB[0]] if kc == 0 else [], a=[PBB[0]] if kc else [])
                yield
                evac(xT[:].rearrange("p a b -> p (a b)"), P0b_[:, :], [PBB[0]], [B.xT])
                yield
                groups = [(0, 512, qkraw[:, 0:512], B.qkraw), (512, 512, qkraw[:, 512:1024], B.qkraw), (2560, 16, g17[:, 0:16], B.g17),
                          (1536, 512, qkb[:], B.qkb), (2048, 512, vbb[:], B.vbb), (1024, 512, vf[:], B.vf),
                          (2576, 512, gates[:, 0:512], Bg), (3088, 512, gates[:, 512:1024], Bg)]
                pend = None
                for gi, (c0, n, dst, db) in enumerate(groups):
                    bank = gi % 2
                    for kc in range(8):
                        S.op("pe", lambda t: t.matmul(PB[bank][:, 0:n], lhsT=xT[:, kc, :], rhs=wbi[:, kc, c0:c0 + n], start=(kc == 0), stop=(kc == 7)),
                             r=[B.xT, B.wbi], w=[PBB[bank]] if kc == 0 else [], a=[PBB[bank]] if kc else [])
                    if pend is not None:
                        evac(*pend, scale=rstd)
                    pend = (dst, PB[bank][:, 0:n], [PBB[bank], B.small], [db])
                    yield
                evac(*pend, scale=rstd)
                yield
                if nxt is not None:
                    load_tile(*nxt[:3])
                S.op("pool", lambda g: g.tensor_copy(out=Vr[:, slot, :, 0:64], in_=vf[:].rearrange("p (h d) -> p h d", h=8)), r=[B.vf], w=[B.get("V%d" % slot)])
                yield

            def a1():
                qk3 = qkraw[:].rearrange("p (h d) -> p h d", h=16)
                S.op("dve", lambda v: v.tensor_tensor(out=tmpA[:], in0=qkraw[:], in1=qkraw[:], op=ALU.mult), r=[B.qkraw], w=[B.tmpA])
                S.op("dve", lambda v: v.tensor_reduce(out=ms16, in_=tmpA[:].rearrange("p (h d) -> p h d", h=16), axis=AX.X, op=ALU.add), r=[B.tmpA], w=[B.small16])
                yield
                rsqrt_chain(rs16, ms16, 1.0 / 64, [B.small16], [B.small16])
                yield
                S.op("dve", lambda v: v.tensor_tensor(out=qk3, in0=qk3, in1=rs16.unsqueeze(2).broadcast_to([128, 16, 64]), op=ALU.mult), r=[B.qkraw, B.small16], w=[B.qkraw])
                qk4 = qkraw[:].rearrange("p (a h d) -> p a h d", a=2, h=8)
                nw4 = C["qknw"][:].rearrange("p (a d) -> p a d", a=2).unsqueeze(2).broadcast_to([128, 2, 8, 64])
                S.op("dve", lambda v: v.tensor_tensor(out=qk4, in0=qk4, in1=nw4, op=ALU.mult), r=[B.qkraw, B.const], w=[B.qkraw])
                yield
                x1 = qk3[:, :, 0:32]
                x2 = qk3[:, :, 32:64]
                cosb = cs[:, 0:32].unsqueeze(1).broadcast_to([128, 16, 32])
                sinb = cs[:, 32:64].unsqueeze(1).broadcast_to([128, 16, 32])
                t1 = tmpA[:, 0:512].rearrange("p (h d) -> p h d", h=16)
                t2 = tmpA[:, 512:1024].rearrange("p (h d) -> p h d", h=16)
                qh3 = qkh[:].rearrange("p (h d) -> p h d", h=16)
                S.op("dve", lambda v: v.tensor_tensor(out=t1, in0=x1, in1=cosb, op=ALU.mult), r=[B.qkraw, B.cs], w=[B.tmpA])
                S.op("dve", lambda v: v.tensor_tensor(out=t2, in0=x2, in1=sinb, op=ALU.mult), r=[B.qkraw, B.cs], w=[B.tmpA])
                S.op("dve", lambda v: v.tensor_tensor(out=qh3[:, :, 0:32], in0=t1, in1=t2, op=ALU.subtract), r=[B.tmpA], w=[B.qkh])
                yield
                S.op("dve", lambda v: v.tensor_tensor(out=t1, in0=x2, in1=cosb, op=ALU.mult), r=[B.qkraw, B.cs], w=[B.tmpA])
                S.op("dve", lambda v: v.tensor_tensor(out=t2, in0=x1, in1=sinb, op=ALU.mult), r=[B.qkraw, B.cs], w=[B.tmpA])
                S.op("dve", lambda v: v.tensor_tensor(out=qh3[:, :, 32:64], in0=t1, in1=t2, op=ALU.add), r=[B.tmpA], w=[B.qkh])
                yield
                if nxt is not None:
                    load_cs(*nxt[:3])
                if samp:
                    S.dma("pool", "kout", kns[:, :], qkh[:, 512:1024], r=[B.qkh])
                    S.dma("pool", "vout", vns[:, :], vf[:], r=[B.vf])
                elif T >= NT - KEEP:
                    r0 = (T - (NT - KEEP)) * 128
                    S.dma("pool", "kout", kwp[b, r0:r0 + 128, :], qkh[:, 512:1024], r=[B.qkh])
                    S.dma("pool", "vout", vwp[b, r0:r0 + 128, :], vf[:], r=[B.vf])
                qkb16 = tmpA[:].bitcast(BF16)[:, 0:1024]
                evac(qkb16, qkh[:], [B.qkh], [B.tmpA])
                yield
                P2b_ = PB[2][:].bitcast(BF16)
                for blk in range(8):
                    S.op("pe", lambda t: t.transpose(out=P2b_[:, blk * 128:(blk + 1) * 128], in_=qkb16[:, blk * 128:(blk + 1) * 128], identity=C["identb"][:]),
                         r=[B.tmpA, B.const], w=[PBB[2]] if blk == 0 else [], a=[PBB[2]] if blk else [])
                yield
                P2v = P2b_.rearrange("p (a b) -> p a b", a=8)
                S.op("act", lambda a: a.copy(out=qT[0:64, :, 0, :], in_=P2v[0:64, 0:4, :]), r=[PBB[2]], w=[Bq])
                S.op("dve", lambda v: v.tensor_copy(out=qT[64:128, :, 1, :], in_=P2v[64:128, 0:4, :]), r=[PBB[2]], a=[Bq])
                evac(kTr[:, :, slot * 128:(slot + 1) * 128], P2v[:, 4:8, :], [PBB[2]], [B.get("K%d" % slot)])
                yield

            def a2():
                mi = 2 if samp else 0
                S.op("pe", lambda t: t.transpose(out=PB[0][0:17, 0:128], in_=g17[:, 0:17], identity=C["ident"][:]), r=[B.g17, B.const], w=[PBB[0]])
                yield
                S.op("act", lambda a: a.copy(out=g17T[:], in_=PB[0][0:17, 0:128]), r=[PBB[0]], w=[B.g17T])
                yield
                S.op("pe", lambda t: t.matmul(PB[0][:, 0:256], lhsT=g17T[:], rhs=C["wgu"][:], start=True, stop=True), r=[B.g17T, B.const], w=[PBB[0]])
                yield
                S.op("act", lambda a: a.activation(out=lgl[:], in_=PB[0][:, 0:256], func=AF.Exp, scale=-1.0), r=[PBB[0]], w=[B.lgl])
                S.op("act", lambda a: a.activation(out=lgl[:], in_=lgl[:], func=AF.Ln, bias=ones_c), r=[B.lgl, B.const], w=[B.lgl])
                yield
                S.op("pe", lambda t: t.matmul(PB[0][:, 256:512], lhsT=C["gm"][:, mi, :], rhs=lgl[:], start=True, stop=True), r=[B.lgl, B.const], w=[PBB[0]])
                S.op("pe", lambda t: t.matmul(PB[1][:, 0:256], lhsT=C["gm"][:, mi + 1, :], rhs=lgl[:], start=True, stop=True), r=[B.lgl, B.const], w=[PBB[1]])
                nd = 16 if samp else 1
                for h in range(4):
                    rhs = C["oh"][:, 1:17] if samp else C["oh"][:, 0:1]
                    S.op("pe", lambda t: t.matmul(PB[1][0:64, 256 + 16 * h:256 + 16 * h + nd], lhsT=lgl[:, 64 * h:64 * h + 64], rhs=rhs, start=False, stop=True, skip_group_check=True),
                         r=[B.lgl, B.const], a=[PBB[1]])
                yield
                S.op("act", lambda a: a.activation(out=EB[:], in_=PB[0][:, 256:512], func=AF.Exp, scale=-1.0 / 16), r=[PBB[0]], w=[B.EB])
                S.op("act", lambda a: a.activation(out=EBi[:], in_=PB[0][:, 256:512], func=AF.Exp, scale=1.0 / 16), r=[PBB[0]], w=[B.EBi])
                S.op("act", lambda a: a.activation(out=EBl[:], in_=PB[1][:, 0:256], func=AF.Exp, scale=-1.0 / 16), r=[PBB[1]], w=[B.EBl])
                S.op("act", lambda a: a.activation(out=dec16[:], in_=PB[1][0:64, 256:320], func=AF.Exp, scale=-1.0 / 16), r=[PBB[1]], w=[B.dec16])
                yield
                S.op("dve", lambda v: v.tensor_tensor(out=EBl[:], in0=EBl[:], in1=EBi[:], op=ALU.mult), r=[B.EBl, B.EBi], w=[B.EBl])
                S.op("dve", lambda v: v.scalar_tensor_tensor(out=qkt[:, 0:256], in0=qkb[:, 0:256], scalar=0.125, in1=EB[:], op0=ALU.mult, op1=ALU.mult), r=[B.qkb, B.EB], w=[B.qkt])
                S.op("dve", lambda v: v.tensor_tensor(out=qkt[:, 256:512], in0=qkb[:, 256:512], in1=EBi[:], op=ALU.mult), r=[B.qkb, B.EBi], w=[B.qkt])
                S.op("dve", lambda v: v.tensor_tensor(out=qkt[:, 512:768], in0=qkb[:, 256:512], in1=EBl[:], op=ALU.mult), r=[B.qkb, B.EBl], w=[B.qkt])
                yield
                for j in range(8):
                    S.op("pe", lambda t: t.transpose(out=P1b[0:64, j * 128:(j + 1) * 128], in_=qkt[:, 64 * j:64 * j + 64], identity=C["identb"][:]),
                         r=[B.qkt, B.const], w=[PBB[1]] if j == 0 else [], a=[PBB[1]] if j else [])
                yield
                S.op("act", lambda a: a.copy(out=qkT4[:].rearrange("p a b -> p (a b)"), in_=P1b[0:64, :]), r=[PBB[1]], w=[B.qkT4])
                yield
                for h in range(4):
                    S.op("pe", lambda t: t.matmul(v4(PB[0][:])[:, h, :], lhsT=qkT4[:, 4 + h, :], rhs=qkT4[:, h, :], start=True, stop=True),
                         r=[B.qkT4], w=[PBB[0]] if h == 0 else [], a=[PBB[0]] if h else [])
                yield
                S.op("dve", lambda v: v.tensor_tensor(out=AT[:], in0=v4(PB[0][:]), in1=C["gmb"][:, 1 if samp else 0, :].unsqueeze(1).broadcast_to([128, 4, 128]), op=ALU.mult),
                     r=[PBB[0], B.const], w=[B.AT])
                yield
                if not samp:
                    for h in range(4):
                        S.op("pe", lambda t: t.matmul(v4(PB[1][:])[:, h, :], lhsT=AT[:, h, :], rhs=vbb[:, 128 * h:128 * h + 128], start=True, stop=False),
                             r=[B.AT, B.vbb], w=[PBB[1]] if h == 0 else [], a=[PBB[1]] if h else [])
                        S.op("pe", lambda t: t.matmul(v4(PB[1][:])[:, h, :], lhsT=qkT4[:, h, :], rhs=Sb[:, h, :], start=False, stop=True),
                             r=[B.qkT4, B.Sb], a=[PBB[1]])
                    yield
                    for h in range(4):
                        S.op("pe", lambda t: t.matmul(v4(PB[0][:])[0:64, h, :], lhsT=qkt[:, 512 + 64 * h:512 + 64 * h + 64], rhs=vbb[:, 128 * h:128 * h + 128], start=True, stop=True),
                             r=[B.qkt, B.vbb], w=[PBB[0]] if h == 0 else [], a=[PBB[0]] if h else [])
                    yield
                    for h in range(4):
                        S.op("dve", lambda v: v.scalar_tensor_tensor(out=Sst[:, h, :], in0=Sst[:, h, :], scalar=dec16[:, 16 * h:16 * h + 1], in1=v4(PB[0][:])[0:64, h, :],
                                                                       op0=ALU.mult, op1=ALU.add), r=[B.Sst, B.dec16, PBB[0]], w=[B.Sst])
                    yield
                    S.op("act", lambda a: a.copy(out=Sb[:], in_=Sst[:]), r=[B.Sst], w=[B.Sb])
                    if T == NT - 1:
                        S.dma("pool", "sout", sgp[b].rearrange("h k v -> k h v"), Sst[:], r=[B.Sst])
                    yield
                else:
                    UB = [0, 0]
                    S0s = [(S0[:], B.S0, "S0"), (xt[0:64, :].rearrange("p (s v) -> p s v", s=8), B.xt, "S0x")]
                    itc = 0
                    for h in range(4):
                        S.op("pe", lambda t: t.matmul(v4(PB[1][:])[:, h, :], lhsT=AT[:, h, :], rhs=vbb[:, 128 * h:128 * h + 128], start=True, stop=False),
                             r=[B.AT, B.vbb], w=[PBB[1]] if h == 0 else [], a=[PBB[1]] if h else [])
                        for hf in range(2):
                            S0c, S0B, S0n = S0s[itc % 2]
                            itc += 1
                            S.dma("sp", S0n, S0c, sg[8 * hf:8 * hf + 8, h].rearrange("s k v -> k s v"), w=[S0B])
                            S.op("act", lambda a: a.copy(out=S0b[:], in_=S0c), r=[S0B], w=[B.S0b])
                            S.op("dve", lambda v: v.tensor_tensor(out=Zq[:], in0=qkT4[:, h, :].unsqueeze(1).broadcast_to([64, 8, 128]), in1=C["bmask"][:, 8 * hf:8 * hf + 8, :], op=ALU.mult),
                                 r=[B.qkT4, B.const], w=[B.Zq])
                            for s in range(8):
                                S.op("pe", lambda t: t.matmul(v4(PB[1][:])[:, h, :], lhsT=Zq[:, s, :], rhs=S0b[:, s, :], start=False, stop=(hf == 1 and s == 7)),
                                     r=[B.Zq, B.S0b], a=[PBB[1]])
                            S.op("dve", lambda v: v.tensor_tensor(out=VBe[:], in0=vbb[:, 128 * h:128 * h + 128].unsqueeze(1).broadcast_to([128, 8, 128]),
                                                                    in1=C["ohb"][:, 8 * hf:8 * hf + 8].unsqueeze(2).broadcast_to([128, 8, 128]), op=ALU.mult),
                                 r=[B.vbb, B.const], w=[B.VBe])
                            for jj in range(2):
                                ub = UB[jj]
                                S.op("pe", lambda t: t.matmul(PB[ub][0:64, :], lhsT=qkt[:, 512 + 64 * h:512 + 64 * h + 64], rhs=VBe[:, 4 * jj:4 * jj + 4, :].rearrange("p a b -> p (a b)"), start=True, stop=True),
                                     r=[B.qkt, B.VBe], w=[PBB[ub]])
                                s0v = S0c[:, 4 * jj:4 * jj + 4, :]
                                dsl = dec16[:, 16 * h + 8 * hf + 4 * jj:16 * h + 8 * hf + 4 * jj + 4].unsqueeze(2).broadcast_to([64, 4, 128])
                                S.op("dve", lambda v: v.tensor_tensor(out=s0v, in0=s0v, in1=dsl, op=ALU.mult), r=[S0B, B.dec16, B.S0b], w=[S0B])
                                S.op("dve", lambda v: v.tensor_tensor(out=s0v, in0=s0v, in1=v4(PB[ub][0:64, :]), op=ALU.add), r=[S0B, PBB[ub]], w=[S0B])
                            S.dma("pool", "S0o" + S0n, sgs[8 * hf:8 * hf + 8, h].rearrange("s k v -> k s v"), S0c, r=[S0B])
                            yield
                S.op("act", lambda a: a.activation(out=tmpS[:, 0:512], in_=PB[1][:], func=AF.Square), r=[PBB[1]], w=[B.tmpS])
                yield
                S.op("dve", lambda v: v.tensor_reduce(out=ms4, in_=v4(tmpS[:, 0:512]), axis=AX.X, op=ALU.add), r=[B.tmpS], w=[B.small4])
                yield
                rsqrt_chain(rs4, ms4, 1.0 / 128, [B.small4], [B.small4])
                yield
                for h in range(4):
                    S.op("dve", lambda v: v.scalar_tensor_tensor(out=mixed[:, 512 + 128 * h:512 + 128 * h + 128], in0=v4(PB[1][:])[:, h, :], scalar=rs4[:, h:h + 1],
                                                                   in1=gates[:, 512 + 128 * h:512 + 128 * h + 128], op0=ALU.mult, op1=ALU.mult),
                         r=[PBB[1], B.small4, Bg], w=[Bm])
                yield


                yield

            def a3():
                S.op("act", lambda a: a.activation(out=tmpS[:], in_=gates[:], func=AF.Exp, scale=-1.0), r=[Bg], w=[B.tmpS])
                S.op("act", lambda a: a.activation(out=tmpS[:], in_=tmpS[:], func=AF.Ln, bias=ones_c), r=[B.tmpS, B.const], w=[B.tmpS])
                yield
                S.op("act", lambda a: a.activation(out=tmpS[:], in_=tmpS[:], func=AF.Exp, scale=-1.0), r=[B.tmpS], w=[B.tmpS])
                yield
                S.op("pool", lambda g: g.tensor_tensor(out=gates[:], in0=gates[:], in1=tmpS[:], op=ALU.mult), r=[Bg, B.tmpS], w=[Bg])
                S.op("pool", lambda g: g.tensor_tensor(out=v4(gates[:, 512:1024]), in0=v4(gates[:, 512:1024]), in1=C["gnw"][:].unsqueeze(1).broadcast_to([128, 4, 128]), op=ALU.mult),
                     r=[Bg, B.const], w=[Bg])
                yield
                yield

            def empty():
                return
                yield
            if SKIPG:
                return a0(), (empty() if SKIPG & 1 else a1()), (empty() if SKIPG & 2 else a2()), (empty() if SKIPG & 4 else a3())
            return a0(), a1(), a2(), a3()
        def stageB(kind, b, T, par):
            samp = kind == "s"
            gates, qT, mixed = gates_[par], qT_[par], mixed_[par]
            Bg, Bq, Bm = B.get("gates%d" % par), B.get("qT%d" % par), B.get("mixed%d" % par)
            xsrc = xs[:, :] if samp else xp[(b * NT + T) * 128:(b * NT + T + 1) * 128, :]
            ydst = ys[:, :] if samp else yp[(b * NT + T) * 128:(b * NT + T + 1) * 128, :]
            S.dma("sp", "ysb", ysb[:], xsrc, w=[B.ysb])
            o3 = [PB[6][:, 0:264].rearrange("p (a b) -> p a b", a=4), PB[7][:, 0:264].rearrange("p (a b) -> p a b", a=4)]
            if not samp:
                clist = list(range(max(0, T - 16), T + 1))
                items = [(idx, c, hg) for idx, c in enumerate(clist) for hg in range(2)]

                SB = [3, 4, 5]

                def st_mm(it):
                    idx, c, hg = items[it]
                    sc = gslot("p", b, c)
                    bk = SB[it % 3]
                    for jp in range(2):
                        pr = hg * 2 + jp
                        S.op("pe", lambda t: t.matmul(PB[bk][:, 256 * jp:256 * jp + 256], lhsT=kTr[:, pr, sc * 128:(sc + 1) * 128], rhs=qT[:, pr, :, :].rearrange("p a b -> p (a b)"), start=True, stop=True),
                             r=[B.get("K%d" % sc), Bq], w=[PBB[bk]] if jp == 0 else [], a=[PBB[bk]] if jp else [])

                n_it = len(items)

                def ex_op(it):
                    bk = SB[it % 3]
                    S.op("act", lambda a: a.activation(out=pTs[it % 3][:], in_=v4(PB[bk][:]), func=AF.Exp, scale=0.125), r=[PBB[bk]], w=[pTsB[it % 3]])

                def mk_op(it):
                    idx, c, hg = items[it]
                    pt, pb = pTs[it % 3], pTsB[it % 3]
                    me = "pool" if (MPOOL and it % MPOOL == MPOOL - 1) else "dve"
                    S.op(me, lambda v: v.tensor_tensor(out=pt[:], in0=pt[:], in1=C["mp"][:, T - c, :].unsqueeze(1).broadcast_to([128, 4, 128]), op=ALU.mult), r=[pb, B.const], w=[pb])

                def pv_op(it):
                    idx, c, hg = items[it]
                    sc = gslot("p", b, c)
                    pt, pb = pTs[it % 3], pTsB[it % 3]
                    for j in range(4):
                        h = hg * 4 + j
                        first = (idx == 0 and j == 0)
                        S.op("pe", lambda t: t.matmul(o3[hg][:, j, :], lhsT=pt[:, j, :], rhs=Vr[:, sc, h, :], start=first, stop=(idx == len(clist) - 1), skip_group_check=True),
                             r=[pb, B.get("V%d" % sc)], w=[PBB[6 + hg]] if first else [], a=[] if first else [PBB[6 + hg]])

                st_mm(0)
                if n_it > 1:
                    st_mm(1)
                for it in range(n_it + 2):
                    if it < n_it:
                        ex_op(it)
                    if it + 2 < n_it:
                        st_mm(it + 2)
                    if 0 <= it - 1 < n_it:
                        mk_op(it - 1)
                    if 0 <= it - 2 < n_it:
                        pv_op(it - 2)
                    yield
            else:
                OT = [PB[6][0:66, :].rearrange("p (a b) -> p a b", a=4), PB[7][0:66, :].rearrange("p (a b) -> p a b", a=4)]
                groups_c = [[0, 1, 2, 3], [4, 5, 6, 7], [8, 9], [10]]
                gl = [(s_, gi) for s_ in range(NSS) for gi in range(4)]
                ptv = [pTs[k][:, 0:2, :].rearrange("p a b -> p (a b)") for k in range(3)]
                ptb = pTsB
                SBK = [4, 5, 0]

                S.barrier()
                kstv = [qkraw[:, 0:512], qkraw[:, 512:1024], qkh[:, 0:512], qkh[:, 512:1024]]
                vstv = [tmpA[:, 0:512], tmpA[:, 512:1024], xt[:, 0:512], xt[:, 512:1024]]
                PT2 = [PB[2], PB[3]]
                PT2b = [PB[2][:].bitcast(BF16), PB[3][:].bitcast(BF16)]
                kbv = [gates_[1 - par][:].bitcast(BF16)[:, 512 * k_:512 * k_ + 512] for k_ in range(4)]

                def prep_st(n):
                    s_, gi = gl[n]
                    cs_ = groups_c[gi]
                    bk = SBK[n % 3]
                    for ci, c in enumerate(cs_):
                        if c < 10:
                            sc = cslot(c)
                            kb_, vb_ = B.get("kstv%d" % (c % 4)), B.get("vstv%d" % (c % 4))
                            kv, vv = kstv[c % 4], vstv[c % 4]
                            if c < 6:
                                ksrc = ck[s_, 256 * c:256 * c + 256, :].rearrange("(a r) d -> r a d", r=16)[0:8, :, :]
                                vsrc = cv[s_, 256 * c:256 * c + 256, :].rearrange("(a r) d -> r a d", r=16)[0:8, :, :]
                                S.dma("sp", "kstv%d" % (c % 4), kv, ksrc, w=[kb_])
                                S.dma("sp", "vstv%d" % (c % 4), vv, vsrc, w=[vb_])
                            else:
                                r0_ = 128 * (12 + c - 6)
                                S.dma("sp", "kstv%d" % (c % 4), kv, ck[s_, r0_:r0_ + 128, :], w=[kb_])
                                S.dma("sp", "vstv%d" % (c % 4), vv, cv[s_, r0_:r0_ + 128, :], w=[vb_])
                            tb = c % 2
                            kbf, kbB = kbv[c % 4], B.get("kbf%d" % (c % 4))
                            evac(kbf, kv, [kb_], [kbB])
                            for j in range(4):
                                S.op("pe", lambda t: t.transpose(out=PT2b[tb][:, j * 128:(j + 1) * 128], in_=kbf[:, j * 128:(j + 1) * 128], identity=C["identb"][:]),
                                     r=[kbB, B.const], w=[PBB[2 + tb]] if j == 0 else [], a=[PBB[2 + tb]] if j else [])
                            evac(kTr[:, :, sc * 128:(sc + 1) * 128], PT2b[tb][:, 0:512].rearrange("p (a b) -> p a b", a=4), [PBB[2 + tb]], [B.get("K%d" % sc)])
                            evac(Vr[:, sc, :, 0:64], vv.rearrange("p (h d) -> p h d", h=8), [vb_], [B.get("V%d" % sc)])
                        else:
                            sc = SLOT_S
                        for pr in range(4):
                            first = (ci == 0 and pr == 0)
                            S.op("pe", lambda t: t.matmul(PB[bk][:, 64 * ci + 16 * pr:64 * ci + 16 * pr + 16], lhsT=kTr[:, pr, sc * 128:(sc + 1) * 128], rhs=qT[:, pr, :, 8 * s_:8 * s_ + 8], start=True, stop=True),
                                 r=[B.get("K%d" % sc), Bq], w=[PBB[bk]] if first else [], a=[] if first else [PBB[bk]])

                def ex_s(n):
                    s_, gi = gl[n]
                    W = 64 * len(groups_c[gi])
                    bk = SBK[n % 3]
                    S.op("act", lambda a: a.activation(out=ptv[n % 3][:, 0:W], in_=PB[bk][:, 0:W], func=AF.Exp, scale=0.125), r=[PBB[bk]], w=[ptb[n % 3]])

                def mk_s(n):
                    s_, gi = gl[n]
                    cs_ = groups_c[gi]
                    nc_ = len(cs_)
                    W = 64 * nc_
                    pt, pb = ptv[n % 3], ptb[n % 3]
                    m0 = cs_[0] if gi < 3 else 10 + s_
                    S.op("dve", lambda v: v.tensor_tensor(out=pt[:, 0:W].rearrange("p (c h q) -> p c h q", c=nc_, h=8), in0=pt[:, 0:W].rearrange("p (c h q) -> p c h q", c=nc_, h=8),
                                                            in1=C["ms"][:, m0:m0 + nc_, :].unsqueeze(2).broadcast_to([128, nc_, 8, 8]), op=ALU.mult), r=[pb, B.const], w=[pb])

                def pv_s(n):
                    s_, gi = gl[n]
                    cs_ = groups_c[gi]
                    nc_ = len(cs_)
                    pt, pb = ptv[n % 3], ptb[n % 3]
                    for ci, c in enumerate(cs_):
                        sc = cslot(c) if c < 10 else SLOT_S
                        for h in range(8):
                            hg, j = h // 4, h % 4
                            first = (n == 0 and ci == 0 and j == 0)
                            last = (n == len(gl) - 1 and ci == nc_ - 1)
                            S.op("pe", lambda t: t.matmul(OT[hg][:, j, 8 * s_:8 * s_ + 8], lhsT=Vr[:, sc, h, :], rhs=pt[:, 64 * ci + 8 * h:64 * ci + 8 * h + 8], start=first, stop=last, skip_group_check=True),
                                 r=[pb, B.get("V%d" % sc)], w=[PBB[6 + hg]] if first else [], a=[] if first else [PBB[6 + hg]])

                NG = len(gl)
                prep_st(0)
                if NG > 1:
                    prep_st(1)
                for n in range(NG + 2):
                    if n < NG:
                        ex_s(n)
                    if 0 <= n - 2 < NG:
                        pv_s(n - 2)
                    if n + 2 < NG:
                        prep_st(n + 2)
                    if 0 <= n - 1 < NG:
                        mk_s(n - 1)
                    yield
                for hg in range(2):
                    S.op("act" if hg == 0 else "dve", (lambda a: a.copy(out=tmpS[0:66, 512 * hg:512 * hg + 512], in_=PB[6 + hg][0:66, :])) if hg == 0 else
                         (lambda v: v.tensor_copy(out=tmpS[0:66, 512 * hg:512 * hg + 512], in_=PB[6 + hg][0:66, :])), r=[PBB[6 + hg]], w=[B.tmpS] if hg == 0 else [], a=[B.tmpS] if hg else [])
                yield
                o3 = [PB[4][:, 0:264].rearrange("p (a b) -> p a b", a=4), PB[5][:, 0:264].rearrange("p (a b) -> p a b", a=4)]
                for h in range(8):
                    hg, j = h // 4, h % 4
                    S.op("pe", lambda t: t.transpose(out=o3[hg][:, j, :], in_=tmpS[0:66, 128 * h:128 * h + 128], identity=C["ident"][0:66, 0:66]),
                         r=[B.tmpS, B.const], w=[PBB[4 + hg]] if j == 0 else [], a=[PBB[4 + hg]] if j else [])
                yield
                OB = [PBB[4], PBB[5]]
            if not samp:
                OB = [PBB[6], PBB[7]]
            for hg in range(2):
                S.op("dve", lambda v: v.reciprocal(out=rden[:, 4 * hg:4 * hg + 4], in_=o3[hg][:, :, 64]), r=[OB[hg]], w=[B.smallB])
            yield
            for h in range(8):
                S.op("dve", lambda v: v.scalar_tensor_tensor(out=mixed[:, 64 * h:64 * h + 64], in0=o3[h // 4][:, h % 4, 0:64], scalar=rden[:, h:h + 1],
                                                               in1=gates[:, 64 * h:64 * h + 64], op0=ALU.mult, op1=ALU.mult),
                     r=[OB[h // 4], B.smallB, Bg], w=[Bm])
                if h % 4 == 3:
                    yield
            for j in range(8):
                S.op("pe", lambda t: t.transpose(out=P4b[:, j * 128:(j + 1) * 128], in_=mixed[:, j * 128:(j + 1) * 128], identity=C["identb"][:]),
                     r=[Bm, B.const], w=[PBB[4]] if j == 0 else [], a=[PBB[4]] if j else [])
            evac(mT[:].rearrange("p a b -> p (a b)"), P4b[:, :], [PBB[4]], [B.mT])
            yield
            for g in range(2):
                bk = 5 - g
                for fc in range(8):
                    S.op("pe", lambda t: t.matmul(PB[bk][:], lhsT=mT[:, fc, :], rhs=wbo[:, fc, 512 * g:512 * g + 512], start=(fc == 0), stop=(fc == 7)),
                         r=[B.mT, B.wbo], w=[PBB[bk]] if fc == 0 else [], a=[PBB[bk]] if fc else [])
                S.op("dve", lambda v: v.tensor_tensor(out=ysb[:, 512 * g:512 * g + 512], in0=PB[bk][:], in1=ysb[:, 512 * g:512 * g + 512], op=ALU.add), r=[PBB[bk], B.ysb], w=[B.ysb])
                yield
            S.dma("pool", "yout", ydst, ysb[:], r=[B.ysb])
            yield

        order = [("p", b, T) for b in range(NSEQ) for T in range(NT)]
        if SAMPLE:
            order.append(("s", 0, 0))
        N = len(order)

        def step(g, n):
            for _ in range(n):
                try:
                    next(g)
                except StopIteration:
                    return False
            return True

        load_tile(*order[0])
        load_cs(*order[0])
        for b in range(1):
            S.op("pool", lambda g: g.memset(Sst[:], 0.0), w=[B.Sst])
            S.op("pool", lambda g: g.memset(Sb[:], 0.0), w=[B.Sb])

        def mkA(i):
            kind, b, T = order[i]
            if kind == "p" and T == 0 and i > 0:
                S.op("pool", lambda g: g.memset(Sst[:], 0.0), w=[B.Sst])
                S.op("pool", lambda g: g.memset(Sb[:], 0.0), w=[B.Sb])
            return stageA(kind, b, T, i % 2, order[i + 1] if i + 1 < N else None)

        def drive(gB, gens):
            g0, g1, g2, g3 = gens if gens is not None else (None, None, None, None)
            live = {"b": gB is not None, "0": g0 is not None, "1": g1 is not None, "2": g2 is not None, "3": g3 is not None}
            n0 = 0
            while any(live.values()):
                if live["b"]:
                    live["b"] = step(gB, 1)
                if live["0"]:
                    k0 = A0HEAD if n0 < 6 else A0RATIO
                    live["0"] = step(g0, k0)
                    n0 += k0
                    if n0 >= A1START and live["1"]:
                        live["1"] = step(g1, 1)
                    if n0 >= A2START and live["2"]:
                        live["2"] = step(g2, 1)
                else:
                    if live["2"]:
                        live["2"] = step(g2, A2RATE)
                    if live["1"]:
                        live["1"] = step(g1, 1)
                    if live["3"]:
                        live["3"] = step(g3, 1)

        drive(None, mkA(0))
        for i in range(N):
            kind, b, T = order[i]
            drive(stageB(kind, b, T, i % 2), mkA(i + 1) if i + 1 < N else None)
        S.wait_all("sp")
    return nc


def core_inputs(c, x_prompt, x_sample, cache_k_win, cache_v_win, state_gla, norm_w, w_in, w_gate_up, b_gate,
                q_norm_w, k_norm_w, gla_norm_w, w_out, consts, NSEQ=2, NT=32):
    f = lambda a: np.ascontiguousarray(a, dtype=np.float32)
    m = dict(consts)
    m["xp"] = f(x_prompt[NSEQ * c:NSEQ * c + NSEQ, :NT * 128].reshape(NSEQ * NT * 128, 1024))
    m["xs"] = f(x_sample[16 * c:16 * c + 16].reshape(128, 1024))
    m["ck"] = f(cache_k_win[0, 16 * c:16 * c + 16].reshape(16, 2048, 512))
    m["cv"] = f(cache_v_win[0, 16 * c:16 * c + 16].reshape(16, 2048, 512))
    m["sg"] = f(state_gla[0, 16 * c:16 * c + 16])
    m["w_in"] = f(w_in[0])
    m["w_out"] = f(w_out[0])
    m["nw"] = f(norm_w[0].reshape(8, 128).T)
    m["qknw"] = f(np.broadcast_to(np.concatenate([q_norm_w[0], k_norm_w[0]])[None], (128, 128)))
    m["gnw"] = f(np.broadcast_to(gla_norm_w[0][None], (128, 128)))
    m["wgu"] = f(np.concatenate([w_gate_up[0], b_gate[0][None]], axis=0))
    return m


_CACHE = {}


def kernel(x_prompt, x_sample, cache_k_win, cache_v_win, state_gla, norm_w, w_in, w_gate_up, b_gate,
           q_norm_w, k_norm_w, gla_norm_w, w_out):
    args = [np.asarray(a) for a in (x_prompt, x_sample, cache_k_win, cache_v_win, state_gla, norm_w, w_in, w_gate_up,
                                    b_gate, q_norm_w, k_norm_w, gla_norm_w, w_out)]
    if "nc" not in _CACHE:
        _CACHE["nc"] = build()
        _CACHE["consts"] = make_consts(32)
    nc, consts = _CACHE["nc"], _CACHE["consts"]
    in_maps = [core_inputs(c, *args, consts) for c in range(8)]
    res = run_bass_kernel_spmd(nc, in_maps, core_ids=list(range(8))).results
    cat = lambda k: np.concatenate([r[k] for r in res], axis=0)
    y_p = cat("yp").reshape(16, 4096, 1024)
    y_s = cat("ys").reshape(128, 8, 1024)
    kwp = cat("kwp").reshape(1, 16, 2048, 8, 64)
    vwp = cat("vwp").reshape(1, 16, 2048, 8, 64)
    sgp = cat("sgp").reshape(1, 16, 4, 64, 128)
    kns = cat("kns").reshape(1, 128, 8, 8, 64)
    vns = cat("vns").reshape(1, 128, 8, 8, 64)
    sgs = cat("sgs").reshape(1, 128, 4, 64, 128)
    return (y_p, y_s, kwp, vwp, sgp, kns, vns, sgs)
```

```python
import numpy as np
from contextlib import ExitStack
import ml_dtypes
import concourse.bass as bass
import concourse.mybir as mybir
from concourse.bass_utils import run_bass_kernel_spmd
from concourse.alu_op_type import AluOpType as ALU

F32 = mybir.dt.float32
BF16 = mybir.dt.bfloat16
AF = mybir.ActivationFunctionType
AX = mybir.AxisListType
NPBF = ml_dtypes.bfloat16
RING = 18
KSTOP = 0
KATT = 9
ARATIO = 0
NA_EST = 60
A0RATIO = 1
A1START = 9
A0HEAD = 1
SKIPG = 0
EVM = 2
A2START = 999
A2RATE = 1
MPOOL = 0
EPS = 1e-6


class Buf:
    def __init__(self, name):
        self.name = name
        self.w = None
        self.r = []


class Bufs:
    def __init__(self):
        self.d = {}

    def __getattr__(self, k):
        d = self.__dict__["d"]
        if k not in d:
            d[k] = Buf(k)
        return d[k]

    def get(self, k):
        return getattr(self, k)


class Sched:
    def __init__(self, nc, stack):
        self.nc = nc
        self.stack = stack
        self.eng = {"pe": nc.tensor, "act": nc.scalar, "dve": nc.vector, "pool": nc.gpsimd, "sp": nc.sync}
        self.sems = {}
        self.cnt = {}
        self.known = {e: {} for e in self.eng}
        for e in self.eng:
            self._sem("E_" + e)

    def _sem(self, key):
        if key not in self.sems:
            self.sems[key] = self.stack.enter_context(self.nc.semaphore(key))
            self.cnt[key] = 0
        return self.sems[key]

    def _deps(self, e, reads, writes, acc=()):
        deps = {}

        def add(d):
            if d is None:
                return
            k, v = d
            if e == "pe" and k == "E_pe":
                return
            if deps.get(k, 0) < v:
                deps[k] = v
        for b in reads:
            add(b.w)
        for b in writes:
            add(b.w)
            for r in b.r:
                add(r)
        for b in acc:
            add(b.w)
            for r in b.r:
                add(r)
        for k, v in deps.items():
            if self.known[e].get(k, 0) < v:
                self.eng[e].wait_ge(self.sems[k], v)
                self.known[e][k] = v

    def op(self, e, fn, r=(), w=(), a=()):
        self._deps(e, r, w, a)
        inst = fn(self.eng[e])
        key = "E_" + e
        self.cnt[key] += 1
        inst.then_inc(self.sems[key], 1)
        tok = (key, self.cnt[key])
        for b in w:
            b.w = tok
            b.r = []
        for b in a:
            b.w = tok
        for b in r:
            b.r.append(tok)
        return tok

    def dma(self, e, key, out, in_, r=(), w=()):
        self._deps(e, r, w)
        sem = self._sem("D_" + key)
        inst = self.eng[e].dma_start(out=out, in_=in_)
        self.cnt["D_" + key] += 16
        inst.then_inc(sem, 16)
        tok = ("D_" + key, self.cnt["D_" + key])
        for b in w:
            b.w = tok
            b.r = []
        for b in r:
            b.r.append(tok)
        return tok

    def barrier(self):
        for e in self.eng:
            self.wait_all(e)

    def wait_all(self, e):
        for k, v in self.cnt.items():
            if v > 0 and self.known[e].get(k, 0) < v:
                self.eng[e].wait_ge(self.sems[k], v)
                self.known[e][k] = v


def make_consts(NT):
    c = {}
    c["ident"] = np.eye(128, dtype=np.float32)
    c["identb"] = np.eye(128, dtype=np.float32).astype(NPBF)
    k = np.arange(128)[:, None, None]
    d = np.arange(17)[None, :, None]
    q = np.arange(128)[None, None, :]
    dist = 128 * d + q - k
    m = ((dist >= 0) & (dist <= 128)).astype(np.float32)
    m += ((dist >= 0) & (dist <= 512) & (dist % 4 == 0))
    m += ((dist >= 0) & (dist <= 2048) & (dist % 16 == 0))
    c["mp"] = m.astype(NPBF)
    def cnt(dist):
        m_ = ((dist >= 0) & (dist <= 128)).astype(np.float32)
        m_ += ((dist >= 0) & (dist <= 512) & (dist % 4 == 0))
        m_ += ((dist >= 0) & (dist <= 2048) & (dist % 16 == 0))
        return m_
    p_ = np.arange(128)[:, None, None]
    t = np.arange(8)[None, None, :]
    mm_ = np.arange(6)[None, :, None]
    rows_c = 256 * mm_ + 16 * (p_ % 16) + (p_ // 16)
    m_c = cnt(2048 + t - rows_c)
    cc = np.arange(12, 16)[None, :, None]
    m_t = cnt(2048 + t - (128 * cc + p_))
    s_ = np.arange(16)[None, :, None]
    tk = p_ - 8 * s_
    dist = t - tk
    m_n = ((tk >= 0) & (tk < 8) & (dist >= 0)).astype(np.float32) * (1.0 + (dist % 4 == 0) + (dist % 16 == 0))
    c["ms"] = np.concatenate([m_c, m_t, m_n], axis=1).astype(NPBF)
    a = np.arange(128)
    up = (a[:, None] <= a[None, :]).astype(np.float32)
    same = (a[:, None] // 8 == a[None, :] // 8).astype(np.float32)
    c["gm"] = np.stack([up, np.ones((128, 128), np.float32), up * same, same], axis=1)
    c["gmb"] = np.stack([up, up * same], axis=1).astype(NPBF)
    oh = (a[:, None] // 8 == np.arange(16)[None, :]).astype(np.float32)
    c["oh"] = np.concatenate([np.ones((128, 1), np.float32), oh], axis=1)
    c["ohb"] = oh.astype(NPBF)
    bm = (np.arange(16)[:, None] == (a[None, :] // 8)).astype(np.float32)
    c["bmask"] = np.broadcast_to(bm[None], (64, 16, 128)).astype(NPBF).copy()
    half = 32
    inv = (10000.0 ** (-np.arange(half, dtype=np.float32) / half)).astype(np.float32)
    pos = np.concatenate([np.arange(NT * 128, dtype=np.float32), 8192.0 + (np.arange(128) % 8).astype(np.float32)])
    ang = pos[:, None].astype(np.float32) * inv[None, :]
    cs = np.concatenate([np.cos(ang), np.sin(ang)], axis=1).astype(np.float32)
    c["cs"] = cs.reshape(NT + 1, 128, 64).copy()
    return c


CONST_SPECS = [("ident", [128, 128], F32), ("identb", [128, 128], BF16), ("mp", [128, 17, 128], BF16),
               ("ms", [128, 26, 8], BF16), ("gm", [128, 4, 128], F32), ("gmb", [128, 2, 128], BF16),
               ("oh", [128, 17], F32), ("ohb", [128, 16], BF16), ("bmask", [64, 16, 128], BF16),
               ("nw", [128, 8], F32), ("qknw", [128, 128], F32), ("gnw", [128, 128], F32), ("wgu", [17, 256], F32)]


def build(NSEQ=2, NT=32, SAMPLE=True, NSS=16):
    nc = bass.Bass("TRN2", target_bir_lowering=False)
    KEEP = min(16, NT)
    di = lambda n, s, d=F32: nc.dram_tensor(n, s, d, kind="ExternalInput").ap()
    do = lambda n, s, d=F32: nc.dram_tensor(n, s, d, kind="ExternalOutput").ap()
    xp = di("xp", [NSEQ * NT * 128, 1024])
    xs = di("xs", [128, 1024])
    ck = di("ck", [16, 2048, 512])
    cv = di("cv", [16, 2048, 512])
    sg = di("sg", [16, 4, 64, 128])
    w_in = di("w_in", [1024, 3600])
    w_out = di("w_out", [1024, 1024])
    csd = di("cs", [NT + 1, 128, 64])
    cd = {n: di(n, s, d) for n, s, d in CONST_SPECS}
    yp = do("yp", [NSEQ * NT * 128, 1024])
    ys = do("ys", [128, 1024])
    kwp = do("kwp", [NSEQ, KEEP * 128, 512])
    vwp = do("vwp", [NSEQ, KEEP * 128, 512])
    sgp = do("sgp", [NSEQ, 4, 64, 128])
    kns = do("kns", [128, 512])
    vns = do("vns", [128, 512])
    sgs = do("sgs", [16, 4, 64, 128])

    st = ExitStack()
    with st:
        S = Sched(nc, st)
        B = Bufs()
        sb = lambda n, s, d=F32: st.enter_context(nc.sbuf_tensor(n, s, d))
        wbi = sb("wbi", [128, 8, 3600], BF16)
        wbo = sb("wbo", [128, 8, 1024], BF16)
        kTr = sb("kTr", [128, 4, RING * 128], BF16)
        Vr = sb("Vr", [128, RING, 8, 66], BF16)
        C = {n: sb("c_" + n, s, d) for n, s, d in CONST_SPECS}
        xt = sb("xt", [128, 1024])
        cs = sb("cs_t", [128, 64])
        tmpA = sb("tmpA", [128, 1024])
        tmpS = sb("tmpS", [128, 1024])
        xT = sb("xT", [128, 8, 128], BF16)
        qkraw = sb("qkraw", [128, 1024])
        vf = sb("vf", [128, 512])
        qkb = sb("qkb", [128, 512])
        vbb = sb("vbb", [128, 512], BF16)
        g17 = sb("g17", [128, 17])
        gates_ = [sb("gates%d" % i, [128, 1024]) for i in range(2)]
        qkh = sb("qkh", [128, 1024])
        qT_ = [sb("qT%d" % i, [128, 4, 2, 128], BF16) for i in range(2)]
        mixed_ = [sb("mixed%d" % i, [128, 1024], BF16) for i in range(2)]
        mT = sb("mT", [128, 8, 128], BF16)
        ysb = sb("ysb", [128, 1024])
        smallB = sb("smallB", [128, 16])
        small = sb("small", [128, 64])
        dec16 = sb("dec16", [64, 64])
        g17T = sb("g17T", [17, 128])
        lgl = sb("lgl", [128, 256])
        EB = sb("EB", [128, 256])
        EBi = sb("EBi", [128, 256])
        EBl = sb("EBl", [128, 256])
        qkt = sb("qkt", [128, 768], BF16)
        qkT4 = sb("qkT4", [64, 8, 128], BF16)
        AT = sb("AT", [128, 4, 128], BF16)
        Sst = sb("Sst", [64, 4, 128])
        Sb = sb("Sb", [64, 4, 128], BF16)
        if SAMPLE:
            kst = [sb("kst%d" % i, [128, 512]) for i in range(2)]
            vst0 = sb("vst0", [128, 512])
            vst = [vst0, vst0]
            S0 = sb("S0", [64, 8, 128])
            S0b = sb("S0b", [64, 8, 128], BF16)
            Zq = sb("Zq", [64, 8, 128], BF16)
            VBe = sb("VBe", [128, 8, 128], BF16)
        pTpad = [[sb("pTpad%d%d" % (i, j), [128, 4, 128], BF16) for j in range(2)] for i in range(2)]
        pTs = [pTpad[0][0], pTpad[0][1], pTpad[1][0]]
        pTsB = [B.pTpad00, B.pTpad01, B.pTpad10]
        PB = [st.enter_context(nc.psum_tensor("pb%d" % i, [128, 512], F32)) for i in range(8)]
        PBB = [B.get("pb%d" % i) for i in range(8)]
        P1b = PB[1][:].bitcast(BF16)
        P4b = PB[4][:].bitcast(BF16)
        v4 = lambda ap: ap.rearrange("p (a b) -> p a b", a=4)

        ms = small[:, 0:1]
        rstd = small[:, 1:2]
        ms16 = small[:, 2:18]
        rs16 = small[:, 18:34]
        rden = smallB[:, 0:8]
        ms4 = small[:, 42:46]
        rs4 = small[:, 46:50]
        dec = small[0:64, 50:54]
        ones_c = C["oh"][:, 0:1]
        eps_c = sb("eps_c", [128, 1])

        for n, s, d in CONST_SPECS:
            S.dma("sp", "const", C[n][:], cd[n], w=[B.const])
        S.op("pool", lambda g: g.memset(Vr[:, :, :, 64:66], 1.0), w=[B.get("V%d" % i) for i in range(RING)])
        S.op("pool", lambda g: g.memset(g17[:, 16:17], 1.0), w=[B.g17])
        S.op("pool", lambda g: g.memset(small[:, 0:1], 0.0), w=[B.small])
        S.op("pool", lambda g: g.memset(eps_c[:], EPS), w=[B.const2])
        for i in range(2):
            S.op("pool", lambda g: g.memset(qT_[i][:], 0.0), w=[B.get("qT%d" % i)])
        stg_in = [(qkraw, B.qkraw, "qkraw"), (qkh, B.qkh, "qkh"), (tmpA, B.tmpA, "tmpA"), (gates_[0], B.gates0, "gates0"), (gates_[1], B.gates1, "gates1")]
        stg_out = [(tmpS, B.tmpS, "tmpS"), (ysb, B.ysb, "ysb")]
        k_ = 0
        for kc in range(8):
            for (c0, c1) in [(0, 1024), (1024, 2048), (2048, 3072), (3072, 3600)]:
                st_, sb_, sn_ = stg_in[k_ % len(stg_in)]
                k_ += 1
                S.dma("sp", "stage_" + sn_, st_[:, 0:c1 - c0], w_in[kc * 128:(kc + 1) * 128, c0:c1], w=[sb_])
                S.op("dve", lambda v: v.tensor_scalar(out=wbi[:, kc, c0:c1], in0=st_[:, 0:c1 - c0], scalar1=C["nw"][:, kc:kc + 1],
                                                        scalar2=None, op0=ALU.mult), r=[sb_, B.const], w=[B.wbi])
            st_, sb_, sn_ = stg_out[kc % 2]
            S.dma("sp", "stage_" + sn_, st_[:, :], w_out[kc * 128:(kc + 1) * 128, :], w=[sb_])
            S.op("act", lambda a: a.copy(out=wbo[:, kc, :], in_=st_[:, :]), r=[sb_], w=[B.wbo])

        def rsqrt_chain(dst, src, scale, rb, wb):
            S.op("act", lambda a: a.activation(out=dst, in_=src, func=AF.Ln, scale=scale, bias=eps_c[0:dst.shape[0], :]), r=list(rb) + [B.const2], w=wb)
            S.op("act", lambda a: a.activation(out=dst, in_=dst, func=AF.Exp, scale=-0.5), r=wb, w=wb)

        evac_flip = [0]

        def evac(out, in_, rb, wb, scale=None):
            e = "act" if (evac_flip[0] % EVM) != EVM - 1 else "dve"
            evac_flip[0] += 1
            if e == "act":
                if scale is None:
                    S.op("act", lambda a: a.copy(out=out, in_=in_), r=rb, w=wb)
                else:
                    S.op("act", lambda a: a.activation(out=out, in_=in_, func=AF.Copy, scale=scale), r=rb, w=wb)
            else:
                if scale is None:
                    S.op("dve", lambda v: v.tensor_copy(out=out, in_=in_), r=rb, w=wb)
                else:
                    S.op("dve", lambda v: v.tensor_scalar(out=out, in0=in_, scalar1=scale, scalar2=None, op0=ALU.mult), r=rb, w=wb)

        def load_tile(kind, b, T):
            src = xs[:, :] if kind == "s" else xp[(b * NT + T) * 128:(b * NT + T + 1) * 128, :]
            S.dma("sp", "xt", xt[:], src, w=[B.xt])

        def load_cs(kind, b, T):
            S.dma("sp", "cs", cs[:], csd[NT if kind == "s" else T], w=[B.cs])

        def gslot(kind, b, T):
            return (NSEQ * NT if kind == "s" else b * NT + T) % RING

        SLOT_S = gslot("s", 0, 0)
        cslot = lambda c: c if c < SLOT_S else c + 1

        def stageA(kind, b, T, par, nxt):
            samp = kind == "s"
            slot = gslot(kind, b, T)
            gates, qT, mixed = gates_[par], qT_[par], mixed_[par]
            Bg, Bq, Bm = B.get("gates%d" % par), B.get("qT%d" % par), B.get("mixed%d" % par)
            def a0():
                S.op("act", lambda a: a.activation(out=tmpA[:], in_=xt[:], func=AF.Square, scale=1.0 / 32, accum_out=ms), r=[B.xt], w=[B.tmpA, B.small])
                rsqrt_chain(rstd, ms, 1.0, [B.small], [B.small])
                yield
                for half in range(2):
                    for j in range(4):
                        kc = half * 4 + j
                        S.op("pe", lambda t: t.transpose(out=v4(PB[2][:])[:, j, :], in_=xt[:, kc * 128:(kc + 1) * 128], identity=C["ident"][:]),
                             r=[B.xt, B.const], w=[PBB[2]] if j == 0 else [], a=[PBB[2]] if j else [])
                    yield
                    evac(xT[:, half * 4:half * 4 + 4, :], v4(PB[2][:]), [PBB[2]], [B.xT])
                    yield
                groups = [(0, 512, qkraw[:, 0:512], B.qkraw), (512, 512, qkraw[:, 512:1024], B.qkraw), (2560, 16, g17[:, 0:16], B.g17),
                          (1536, 512, qkb[:], B.qkb), (2048, 512, vbb[:], B.vbb), (1024, 512, vf[:], B.vf),
                          (2576, 512, gates[:, 0:512], Bg), (3088, 512, gates[:, 512:1024], Bg)]
                pend = None
                for gi, (c0, n, dst, db) in enumerate(groups):
                    bank = gi % 2
                    for kc in range(8):
                        S.op("pe", lambda t: t.matmul(PB[bank][:, 0:n], lhsT=xT[:, kc, :], rhs=wbi[:, kc, c0:c0 + n], start=(kc == 0), stop=(kc == 7)),
                             r=[B.xT, B.wbi], w=[PBB[bank]] if kc == 0 else [], a=[PBB[bank]] if kc else [])
                    if pend is not None:
                        evac(*pend, scale=rstd)
                    pend = (dst, PB[bank][:, 0:n], [PBB[bank], B.small], [db])
                    yield
                evac(*pend, scale=rstd)
                yield
                if nxt is not None:
                    load_tile(*nxt[:3])
                S.op("pool", lambda g: g.tensor_copy(out=Vr[:, slot, :, 0:64], in_=vf[:].rearrange("p (h d) -> p h d", h=8)), r=[B.vf], w=[B.get("V%d" % slot)])
                S.op("dve", lambda v: v.memset(ms, 0.0), w=[B.small])
                yield

            def a1():
                qk3 = qkraw[:].rearrange("p (h d) -> p h d", h=16)
                S.op("dve", lambda v: v.tensor_tensor(out=tmpA[:], in0=qkraw[:], in1=qkraw[:], op=ALU.mult), r=[B.qkraw], w=[B.tmpA])
                S.op("dve", lambda v: v.tensor_reduce(out=ms16, in_=tmpA[:].rearrange("p (h d) -> p h d", h=16), axis=AX.X, op=ALU.add), r=[B.tmpA], w=[B.small16])
                yield
                rsqrt_chain(rs16, ms16, 1.0 / 64, [B.small16], [B.small16])
                yield
                S.op("dve", lambda v: v.tensor_tensor(out=qk3, in0=qk3, in1=rs16.unsqueeze(2).broadcast_to([128, 16, 64]), op=ALU.mult), r=[B.qkraw, B.small16], w=[B.qkraw])
                qk4 = qkraw[:].rearrange("p (a h d) -> p a h d", a=2, h=8)
                nw4 = C["qknw"][:].rearrange("p (a d) -> p a d", a=2).unsqueeze(2).broadcast_to([128, 2, 8, 64])
                S.op("dve", lambda v: v.tensor_tensor(out=qk4, in0=qk4, in1=nw4, op=ALU.mult), r=[B.qkraw, B.const], w=[B.qkraw])
                yield
                x1 = qk3[:, :, 0:32]
                x2 = qk3[:, :, 32:64]
                cosb = cs[:, 0:32].unsqueeze(1).broadcast_to([128, 16, 32])
                sinb = cs[:, 32:64].unsqueeze(1).broadcast_to([128, 16, 32])
                t1 = tmpA[:, 0:512].rearrange("p (h d) -> p h d", h=16)
                t2 = tmpA[:, 512:1024].rearrange("p (h d) -> p h d", h=16)
                qh3 = qkh[:].rearrange("p (h d) -> p h d", h=16)
                S.op("dve", lambda v: v.tensor_tensor(out=t1, in0=x1, in1=cosb, op=ALU.mult), r=[B.qkraw, B.cs], w=[B.tmpA])
                S.op("dve", lambda v: v.tensor_tensor(out=t2, in0=x2, in1=sinb, op=ALU.mult), r=[B.qkraw, B.cs], w=[B.tmpA])
                S.op("dve", lambda v: v.tensor_tensor(out=qh3[:, :, 0:32], in0=t1, in1=t2, op=ALU.subtract), r=[B.tmpA], w=[B.qkh])
                yield
                S.op("dve", lambda v: v.tensor_tensor(out=t1, in0=x2, in1=cosb, op=ALU.mult), r=[B.qkraw, B.cs], w=[B.tmpA])
                S.op("dve", lambda v: v.tensor_tensor(out=t2, in0=x1, in1=sinb, op=ALU.mult), r=[B.qkraw, B.cs], w=[B.tmpA])
                S.op("dve", lambda v: v.tensor_tensor(out=qh3[:, :, 32:64], in0=t1, in1=t2, op=ALU.add), r=[B.tmpA], w=[B.qkh])
                yield
                if nxt is not None:
                    load_cs(*nxt[:3])
                if samp:
                    S.dma("pool", "kout", kns[:, :], qkh[:, 512:1024], r=[B.qkh])
                    S.dma("pool", "vout", vns[:, :], vf[:], r=[B.vf])
                elif T >= NT - KEEP:
                    r0 = (T - (NT - KEEP)) * 128
                    S.dma("pool", "kout", kwp[b, r0:r0 + 128, :], qkh[:, 512:1024], r=[B.qkh])
                    S.dma("pool", "vout", vwp[b, r0:r0 + 128, :], vf[:], r=[B.vf])
                qkb16 = tmpA[:].bitcast(BF16)[:, 0:1024]
                evac(qkb16, qkh[:], [B.qkh], [B.tmpA])
                yield
                P2b_ = PB[2][:].bitcast(BF16)
                for blk in range(8):
                    S.op("pe", lambda t: t.transpose(out=P2b_[:, blk * 128:(blk + 1) * 128], in_=qkb16[:, blk * 128:(blk + 1) * 128], identity=C["identb"][:]),
                         r=[B.tmpA, B.const], w=[PBB[2]] if blk == 0 else [], a=[PBB[2]] if blk else [])
                yield
                P2v = P2b_.rearrange("p (a b) -> p a b", a=8)
                S.op("act", lambda a: a.copy(out=qT[0:64, :, 0, :], in_=P2v[0:64, 0:4, :]), r=[PBB[2]], w=[Bq])
                S.op("dve", lambda v: v.tensor_copy(out=qT[64:128, :, 1, :], in_=P2v[64:128, 0:4, :]), r=[PBB[2]], a=[Bq])
                evac(kTr[:, :, slot * 128:(slot + 1) * 128], P2v[:, 4:8, :], [PBB[2]], [B.get("K%d" % slot)])
                yield

            def a2():
                mi = 2 if samp else 0
                S.op("pe", lambda t: t.transpose(out=PB[0][0:17, 0:128], in_=g17[:, 0:17], identity=C["ident"][:]), r=[B.g17, B.const], w=[PBB[0]])
                yield
                S.op("act", lambda a: a.copy(out=g17T[:], in_=PB[0][0:17, 0:128]), r=[PBB[0]], w=[B.g17T])
                yield
                S.op("pe", lambda t: t.matmul(PB[0][:, 0:256], lhsT=g17T[:], rhs=C["wgu"][:], start=True, stop=True), r=[B.g17T, B.const], w=[PBB[0]])
                yield
                S.op("act", lambda a: a.activation(out=lgl[:], in_=PB[0][:, 0:256], func=AF.Exp, scale=-1.0), r=[PBB[0]], w=[B.lgl])
                S.op("act", lambda a: a.activation(out=lgl[:], in_=lgl[:], func=AF.Ln, bias=ones_c), r=[B.lgl, B.const], w=[B.lgl])
                yield
                S.op("pe", lambda t: t.matmul(PB[0][:, 256:512], lhsT=C["gm"][:, mi, :], rhs=lgl[:], start=True, stop=True), r=[B.lgl, B.const], w=[PBB[0]])
                S.op("pe", lambda t: t.matmul(PB[1][:, 0:256], lhsT=C["gm"][:, mi + 1, :], rhs=lgl[:], start=True, stop=True), r=[B.lgl, B.const], w=[PBB[1]])
                nd = 16 if samp else 1
                for h in range(4):
                    rhs = C["oh"][:, 1:17] if samp else C["oh"][:, 0:1]
                    S.op("pe", lambda t: t.matmul(PB[1][0:64, 256 + 16 * h:256 + 16 * h + nd], lhsT=lgl[:, 64 * h:64 * h + 64], rhs=rhs, start=False, stop=True, skip_group_check=True),
                         r=[B.lgl, B.const], a=[PBB[1]])
                yield
                S.op("act", lambda a: a.activation(out=EB[:], in_=PB[0][:, 256:512], func=AF.Exp, scale=-1.0 / 16), r=[PBB[0]], w=[B.EB])
                S.op("act", lambda a: a.activation(out=EBi[:], in_=PB[0][:, 256:512], func=AF.Exp, scale=1.0 / 16), r=[PBB[0]], w=[B.EBi])
                S.op("act", lambda a: a.activation(out=EBl[:], in_=PB[1][:, 0:256], func=AF.Exp, scale=-1.0 / 16), r=[PBB[1]], w=[B.EBl])
                S.op("act", lambda a: a.activation(out=dec16[:], in_=PB[1][0:64, 256:320], func=AF.Exp, scale=-1.0 / 16), r=[PBB[1]], w=[B.dec16])
                yield
                S.op("dve", lambda v: v.tensor_tensor(out=EBl[:], in0=EBl[:], in1=EBi[:], op=ALU.mult), r=[B.EBl, B.EBi], w=[B.EBl])
                S.op("dve", lambda v: v.scalar_tensor_tensor(out=qkt[:, 0:256], in0=qkb[:, 0:256], scalar=0.125, in1=EB[:], op0=ALU.mult, op1=ALU.mult), r=[B.qkb, B.EB], w=[B.qkt])
                S.op("dve", lambda v: v.tensor_tensor(out=qkt[:, 256:512], in0=qkb[:, 256:512], in1=EBi[:], op=ALU.mult), r=[B.qkb, B.EBi], w=[B.qkt])
                S.op("dve", lambda v: v.tensor_tensor(out=qkt[:, 512:768], in0=qkb[:, 256:512], in1=EBl[:], op=ALU.mult), r=[B.qkb, B.EBl], w=[B.qkt])
                yield
                for j in range(8):
                    S.op("pe", lambda t: t.transpose(out=P1b[0:64, j * 128:(j + 1) * 128], in_=qkt[:, 64 * j:64 * j + 64], identity=C["identb"][:]),
                         r=[B.qkt, B.const], w=[PBB[1]] if j == 0 else [], a=[PBB[1]] if j else [])
                yield
                S.op("act", lambda a: a.copy(out=qkT4[:].rearrange("p a b -> p (a b)"), in_=P1b[0:64, :]), r=[PBB[1]], w=[B.qkT4])
                yield
                for h in range(4):
                    S.op("pe", lambda t: t.matmul(v4(PB[0][:])[:, h, :], lhsT=qkT4[:, 4 + h, :], rhs=qkT4[:, h, :], start=True, stop=True),
                         r=[B.qkT4], w=[PBB[0]] if h == 0 else [], a=[PBB[0]] if h else [])
                yield
                S.op("dve", lambda v: v.tensor_tensor(out=AT[:], in0=v4(PB[0][:]), in1=C["gmb"][:, 1 if samp else 0, :].unsqueeze(1).broadcast_to([128, 4, 128]), op=ALU.mult),
                     r=[PBB[0], B.const], w=[B.AT])
                yield
                if not samp:
                    for h in range(4):
                        S.op("pe", lambda t: t.matmul(v4(PB[1][:])[:, h, :], lhsT=AT[:, h, :], rhs=vbb[:, 128 * h:128 * h + 128], start=True, stop=False),
                             r=[B.AT, B.vbb], w=[PBB[1]] if h == 0 else [], a=[PBB[1]] if h else [])
                        S.op("pe", lambda t: t.matmul(v4(PB[1][:])[:, h, :], lhsT=qkT4[:, h, :], rhs=Sb[:, h, :], start=False, stop=True),
                             r=[B.qkT4, B.Sb], a=[PBB[1]])
                    yield
                    for h in range(4):
                        S.op("pe", lambda t: t.matmul(v4(PB[0][:])[0:64, h, :], lhsT=qkt[:, 512 + 64 * h:512 + 64 * h + 64], rhs=vbb[:, 128 * h:128 * h + 128], start=True, stop=True),
                             r=[B.qkt, B.vbb], w=[PBB[0]] if h == 0 else [], a=[PBB[0]] if h else [])
                    yield
                    for h in range(4):
                        S.op("dve", lambda v: v.scalar_tensor_tensor(out=Sst[:, h, :], in0=Sst[:, h, :], scalar=dec16[:, 16 * h:16 * h + 1], in1=v4(PB[0][:])[0:64, h, :],
                                                                       op0=ALU.mult, op1=ALU.add), r=[B.Sst, B.dec16, PBB[0]], w=[B.Sst])
                    yield
                    S.op("act", lambda a: a.copy(out=Sb[:], in_=Sst[:]), r=[B.Sst], w=[B.Sb])
                    if T == NT - 1:
                        S.dma("pool", "sout", sgp[b].rearrange("h k v -> k h v"), Sst[:], r=[B.Sst])
                    yield
                else:
                    UB = [0, 0]
                    S0s = [(S0[:], B.S0, "S0"), (xt[0:64, :].rearrange("p (s v) -> p s v", s=8), B.xt, "S0x")]
                    itc = 0
                    for h in range(4):
                        S.op("pe", lambda t: t.matmul(v4(PB[1][:])[:, h, :], lhsT=AT[:, h, :], rhs=vbb[:, 128 * h:128 * h + 128], start=True, stop=False),
                             r=[B.AT, B.vbb], w=[PBB[1]] if h == 0 else [], a=[PBB[1]] if h else [])
                        for hf in range(2):
                            S0c, S0B, S0n = S0s[itc % 2]
                            itc += 1
                            S.dma("sp", S0n, S0c, sg[8 * hf:8 * hf + 8, h].rearrange("s k v -> k s v"), w=[S0B])
                            S.op("act", lambda a: a.copy(out=S0b[:], in_=S0c), r=[S0B], w=[B.S0b])
                            S.op("dve", lambda v: v.tensor_tensor(out=Zq[:], in0=qkT4[:, h, :].unsqueeze(1).broadcast_to([64, 8, 128]), in1=C["bmask"][:, 8 * hf:8 * hf + 8, :], op=ALU.mult),
                                 r=[B.qkT4, B.const], w=[B.Zq])
                            for s in range(8):
                                S.op("pe", lambda t: t.matmul(v4(PB[1][:])[:, h, :], lhsT=Zq[:, s, :], rhs=S0b[:, s, :], start=False, stop=(hf == 1 and s == 7)),
                                     r=[B.Zq, B.S0b], a=[PBB[1]])
                            S.op("dve", lambda v: v.tensor_tensor(out=VBe[:], in0=vbb[:, 128 * h:128 * h + 128].unsqueeze(1).broadcast_to([128, 8, 128]),
                                                                    in1=C["ohb"][:, 8 * hf:8 * hf + 8].unsqueeze(2).broadcast_to([128, 8, 128]), op=ALU.mult),
                                 r=[B.vbb, B.const], w=[B.VBe])
                            for jj in range(2):
                                ub = UB[jj]
                                S.op("pe", lambda t: t.matmul(PB[ub][0:64, :], lhsT=qkt[:, 512 + 64 * h:512 + 64 * h + 64], rhs=VBe[:, 4 * jj:4 * jj + 4, :].rearrange("p a b -> p (a b)"), start=True, stop=True),
                                     r=[B.qkt, B.VBe], w=[PBB[ub]])
                                s0v = S0c[:, 4 * jj:4 * jj + 4, :]
                                dsl = dec16[:, 16 * h + 8 * hf + 4 * jj:16 * h + 8 * hf + 4 * jj + 4].unsqueeze(2).broadcast_to([64, 4, 128])
                                S.op("dve", lambda v: v.tensor_tensor(out=s0v, in0=s0v, in1=dsl, op=ALU.mult), r=[S0B, B.dec16, B.S0b], w=[S0B])
                                S.op("dve", lambda v: v.tensor_tensor(out=s0v, in0=s0v, in1=v4(PB[ub][0:64, :]), op=ALU.add), r=[S0B, PBB[ub]], w=[S0B])
                            S.dma("pool", "S0o" + S0n, sgs[8 * hf:8 * hf + 8, h].rearrange("s k v -> k s v"), S0c, r=[S0B])
                            yield
                S.op("act", lambda a: a.activation(out=tmpS[:, 0:512], in_=PB[1][:], func=AF.Square), r=[PBB[1]], w=[B.tmpS])
                yield
                S.op("dve", lambda v: v.tensor_reduce(out=ms4, in_=v4(tmpS[:, 0:512]), axis=AX.X, op=ALU.add), r=[B.tmpS], w=[B.small4])
                yield
                rsqrt_chain(rs4, ms4, 1.0 / 128, [B.small4], [B.small4])
                yield
                for h in range(4):
                    S.op("dve", lambda v: v.scalar_tensor_tensor(out=mixed[:, 512 + 128 * h:512 + 128 * h + 128], in0=v4(PB[1][:])[:, h, :], scalar=rs4[:, h:h + 1],
                                                                   in1=gates[:, 512 + 128 * h:512 + 128 * h + 128], op0=ALU.mult, op1=ALU.mult),
                         r=[PBB[1], B.small4, Bg], w=[Bm])
                yield


                yield

            def a3():
                S.op("act", lambda a: a.activation(out=tmpS[:], in_=gates[:], func=AF.Exp, scale=-1.0), r=[Bg], w=[B.tmpS])
                S.op("act", lambda a: a.activation(out=tmpS[:], in_=tmpS[:], func=AF.Ln, bias=ones_c), r=[B.tmpS, B.const], w=[B.tmpS])
                yield
                S.op("act", lambda a: a.activation(out=tmpS[:], in_=tmpS[:], func=AF.Exp, scale=-1.0), r=[B.tmpS], w=[B.tmpS])
                yield
                S.op("pool", lambda g: g.tensor_tensor(out=gates[:], in0=gates[:], in1=tmpS[:], op=ALU.mult), r=[Bg, B.tmpS], w=[Bg])
                S.op("pool", lambda g: g.tensor_tensor(out=v4(gates[:, 512:1024]), in0=v4(gates[:, 512:1024]), in1=C["gnw"][:].unsqueeze(1).broadcast_to([128, 4, 128]), op=ALU.mult),
                     r=[Bg, B.const], w=[Bg])
                yield
                yield

            def empty():
                return
                yield
            if SKIPG:
                return a0(), (empty() if SKIPG & 1 else a1()), (empty() if SKIPG & 2 else a2()), (empty() if SKIPG & 4 else a3())
            return a0(), a1(), a2(), a3()
        def stageB(kind, b, T, par):
            samp = kind == "s"
            gates, qT, mixed = gates_[par], qT_[par], mixed_[par]
            Bg, Bq, Bm = B.get("gates%d" % par), B.get("qT%d" % par), B.get("mixed%d" % par)
            xsrc = xs[:, :] if samp else xp[(b * NT + T) * 128:(b * NT + T + 1) * 128, :]
            ydst = ys[:, :] if samp else yp[(b * NT + T) * 128:(b * NT + T + 1) * 128, :]
            S.dma("sp", "ysb", ysb[:], xsrc, w=[B.ysb])
            o3 = [PB[6][:, 0:264].rearrange("p (a b) -> p a b", a=4), PB[7][:, 0:264].rearrange("p (a b) -> p a b", a=4)]
            if not samp:
                clist = list(range(max(0, T - 16), T + 1))
                items = [(idx, c, hg) for idx, c in enumerate(clist) for hg in range(2)]

                SB = [3, 4, 5]

                def st_mm(it):
                    idx, c, hg = items[it]
                    sc = gslot("p", b, c)
                    bk = SB[it % 3]
                    for jp in range(2):
                        pr = hg * 2 + jp
                        S.op("pe", lambda t: t.matmul(PB[bk][:, 256 * jp:256 * jp + 256], lhsT=kTr[:, pr, sc * 128:(sc + 1) * 128], rhs=qT[:, pr, :, :].rearrange("p a b -> p (a b)"), start=True, stop=True),
                             r=[B.get("K%d" % sc), Bq], w=[PBB[bk]] if jp == 0 else [], a=[PBB[bk]] if jp else [])

                n_it = len(items)

                def ex_op(it):
                    bk = SB[it % 3]
                    S.op("act", lambda a: a.activation(out=pTs[it % 3][:], in_=v4(PB[bk][:]), func=AF.Exp, scale=0.125), r=[PBB[bk]], w=[pTsB[it % 3]])

                def mk_op(it):
                    idx, c, hg = items[it]
                    pt, pb = pTs[it % 3], pTsB[it % 3]
                    me = "pool" if (MPOOL and it % MPOOL == MPOOL - 1) else "dve"
                    S.op(me, lambda v: v.tensor_tensor(out=pt[:], in0=pt[:], in1=C["mp"][:, T - c, :].unsqueeze(1).broadcast_to([128, 4, 128]), op=ALU.mult), r=[pb, B.const], w=[pb])

                def pv_op(it):
                    idx, c, hg = items[it]
                    sc = gslot("p", b, c)
                    pt, pb = pTs[it % 3], pTsB[it % 3]
                    for j in range(4):
                        h = hg * 4 + j
                        first = (idx == 0 and j == 0)
                        S.op("pe", lambda t: t.matmul(o3[hg][:, j, :], lhsT=pt[:, j, :], rhs=Vr[:, sc, h, :], start=first, stop=(idx == len(clist) - 1), skip_group_check=True),
                             r=[pb, B.get("V%d" % sc)], w=[PBB[6 + hg]] if first else [], a=[] if first else [PBB[6 + hg]])

                st_mm(0)
                if n_it > 1:
                    st_mm(1)
                for it in range(n_it + 2):
                    if it < n_it:
                        ex_op(it)
                    if it + 2 < n_it:
                        st_mm(it + 2)
                    if 0 <= it - 1 < n_it:
                        mk_op(it - 1)
                    if 0 <= it - 2 < n_it:
                        pv_op(it - 2)
                    yield
            else:
                OT = [PB[6][0:66, :].rearrange("p (a b) -> p a b", a=4), PB[7][0:66, :].rearrange("p (a b) -> p a b", a=4)]
                groups_c = [[0, 1, 2, 3], [4, 5, 6, 7], [8, 9], [10]]
                gl = [(s_, gi) for s_ in range(NSS) for gi in range(4)]
                ptv = [pTs[k][:, 0:2, :].rearrange("p a b -> p (a b)") for k in range(3)]
                ptb = pTsB
                SBK = [4, 5, 0]

                S.barrier()
                kstv = [qkraw[:, 0:512], qkraw[:, 512:1024], qkh[:, 0:512], qkh[:, 512:1024]]
                vstv = [tmpA[:, 0:512], tmpA[:, 512:1024], xt[:, 0:512], xt[:, 512:1024]]
                PT2 = [PB[2], PB[3]]
                PT2b = [PB[2][:].bitcast(BF16), PB[3][:].bitcast(BF16)]
                kbv = [gates_[1 - par][:].bitcast(BF16)[:, 512 * k_:512 * k_ + 512] for k_ in range(4)]

                def prep_st(n):
                    s_, gi = gl[n]
                    cs_ = groups_c[gi]
                    bk = SBK[n % 3]
                    for ci, c in enumerate(cs_):
                        if c < 10:
                            sc = cslot(c)
                            kb_, vb_ = B.get("kstv%d" % (c % 4)), B.get("vstv%d" % (c % 4))
                            kv, vv = kstv[c % 4], vstv[c % 4]
                            if c < 6:
                                ksrc = ck[s_, 256 * c:256 * c + 256, :].rearrange("(a r) d -> r a d", r=16)[0:8, :, :]
                                vsrc = cv[s_, 256 * c:256 * c + 256, :].rearrange("(a r) d -> r a d", r=16)[0:8, :, :]
                                S.dma("sp", "kstv%d" % (c % 4), kv, ksrc, w=[kb_])
                                S.dma("sp", "vstv%d" % (c % 4), vv, vsrc, w=[vb_])
                            else:
                                r0_ = 128 * (12 + c - 6)
                                S.dma("sp", "kstv%d" % (c % 4), kv, ck[s_, r0_:r0_ + 128, :], w=[kb_])
                                S.dma("sp", "vstv%d" % (c % 4), vv, cv[s_, r0_:r0_ + 128, :], w=[vb_])
                            tb = c % 2
                            kbf, kbB = kbv[c % 4], B.get("kbf%d" % (c % 4))
                            evac(kbf, kv, [kb_], [kbB])
                            for j in range(4):
                                S.op("pe", lambda t: t.transpose(out=PT2b[tb][:, j * 128:(j + 1) * 128], in_=kbf[:, j * 128:(j + 1) * 128], identity=C["identb"][:]),
                                     r=[kbB, B.const], w=[PBB[2 + tb]] if j == 0 else [], a=[PBB[2 + tb]] if j else [])
                            evac(kTr[:, :, sc * 128:(sc + 1) * 128], PT2b[tb][:, 0:512].rearrange("p (a b) -> p a b", a=4), [PBB[2 + tb]], [B.get("K%d" % sc)])
                            evac(Vr[:, sc, :, 0:64], vv.rearrange("p (h d) -> p h d", h=8), [vb_], [B.get("V%d" % sc)])
                        else:
                            sc = SLOT_S
                        for pr in range(4):
                            first = (ci == 0 and pr == 0)
                            S.op("pe", lambda t: t.matmul(PB[bk][:, 64 * ci + 16 * pr:64 * ci + 16 * pr + 16], lhsT=kTr[:, pr, sc * 128:(sc + 1) * 128], rhs=qT[:, pr, :, 8 * s_:8 * s_ + 8], start=True, stop=True),
                                 r=[B.get("K%d" % sc), Bq], w=[PBB[bk]] if first else [], a=[] if first else [PBB[bk]])

                def ex_s(n):
                    s_, gi = gl[n]
                    W = 64 * len(groups_c[gi])
                    bk = SBK[n % 3]
                    S.op("act", lambda a: a.activation(out=ptv[n % 3][:, 0:W], in_=PB[bk][:, 0:W], func=AF.Exp, scale=0.125), r=[PBB[bk]], w=[ptb[n % 3]])

                def mk_s(n):
                    s_, gi = gl[n]
                    cs_ = groups_c[gi]
                    nc_ = len(cs_)
                    W = 64 * nc_
                    pt, pb = ptv[n % 3], ptb[n % 3]
                    m0 = cs_[0] if gi < 3 else 10 + s_
                    S.op("dve", lambda v: v.tensor_tensor(out=pt[:, 0:W].rearrange("p (c h q) -> p c h q", c=nc_, h=8), in0=pt[:, 0:W].rearrange("p (c h q) -> p c h q", c=nc_, h=8),
                                                            in1=C["ms"][:, m0:m0 + nc_, :].unsqueeze(2).broadcast_to([128, nc_, 8, 8]), op=ALU.mult), r=[pb, B.const], w=[pb])

                def pv_s(n):
                    s_, gi = gl[n]
                    cs_ = groups_c[gi]
                    nc_ = len(cs_)
                    pt, pb = ptv[n % 3], ptb[n % 3]
                    for ci, c in enumerate(cs_):
                        sc = cslot(c) if c < 10 else SLOT_S
                        for h in range(8):
                            hg, j = h // 4, h % 4
                            first = (n == 0 and ci == 0 and j == 0)
                            last = (n == len(gl) - 1 and ci == nc_ - 1)
                            S.op("pe", lambda t: t.matmul(OT[hg][:, j, 8 * s_:8 * s_ + 8], lhsT=Vr[:, sc, h, :], rhs=pt[:, 64 * ci + 8 * h:64 * ci + 8 * h + 8], start=first, stop=last, skip_group_check=True),
                                 r=[pb, B.get("V%d" % sc)], w=[PBB[6 + hg]] if first else [], a=[] if first else [PBB[6 + hg]])

                NG = len(gl)
                prep_st(0)
                if NG > 1:
                    prep_st(1)
                for n in range(NG + 2):
                    if n < NG:
                        ex_s(n)
                    if 0 <= n - 2 < NG:
                        pv_s(n - 2)
                    if n + 2 < NG:
                        prep_st(n + 2)
                    if 0 <= n - 1 < NG:
                        mk_s(n - 1)
                    yield
                for hg in range(2):
                    S.op("act" if hg == 0 else "dve", (lambda a: a.copy(out=tmpS[0:66, 512 * hg:512 * hg + 512], in_=PB[6 + hg][0:66, :])) if hg == 0 else
                         (lambda v: v.tensor_copy(out=tmpS[0:66, 512 * hg:512 * hg + 512], in_=PB[6 + hg][0:66, :])), r=[PBB[6 + hg]], w=[B.tmpS] if hg == 0 else [], a=[B.tmpS] if hg else [])
                yield
                o3 = [PB[4][:, 0:264].rearrange("p (a b) -> p a b", a=4), PB[5][:, 0:264].rearrange("p (a b) -> p a b", a=4)]
                for h in range(8):
                    hg, j = h // 4, h % 4
                    S.op("pe", lambda t: t.transpose(out=o3[hg][:, j, :], in_=tmpS[0:66, 128 * h:128 * h + 128], identity=C["ident"][0:66, 0:66]),
                         r=[B.tmpS, B.const], w=[PBB[4 + hg]] if j == 0 else [], a=[PBB[4 + hg]] if j else [])
                yield
                OB = [PBB[4], PBB[5]]
            if not samp:
                OB = [PBB[6], PBB[7]]
            for hg in range(2):
                S.op("dve", lambda v: v.reciprocal(out=rden[:, 4 * hg:4 * hg + 4], in_=o3[hg][:, :, 64]), r=[OB[hg]], w=[B.smallB])
            yield
            for h in range(8):
                S.op("dve", lambda v: v.scalar_tensor_tensor(out=mixed[:, 64 * h:64 * h + 64], in0=o3[h // 4][:, h % 4, 0:64], scalar=rden[:, h:h + 1],
                                                               in1=gates[:, 64 * h:64 * h + 64], op0=ALU.mult, op1=ALU.mult),
                     r=[OB[h // 4], B.smallB, Bg], w=[Bm])
                if h % 4 == 3:
                    yield
            for j in range(8):
                S.op("pe", lambda t: t.transpose(out=P4b[:, j * 128:(j + 1) * 128], in_=mixed[:, j * 128:(j + 1) * 128], identity=C["identb"][:]),
                     r=[Bm, B.const], w=[PBB[4]] if j == 0 else [], a=[PBB[4]] if j else [])
            evac(mT[:].rearrange("p a b -> p (a b)"), P4b[:, :], [PBB[4]], [B.mT])
            yield
            for g in range(2):
                bk = 5 - g
                for fc in range(8):
                    S.op("pe", lambda t: t.matmul(PB[bk][:], lhsT=mT[:, fc, :], rhs=wbo[:, fc, 512 * g:512 * g + 512], start=(fc == 0), stop=(fc == 7)),
                         r=[B.mT, B.wbo], w=[PBB[bk]] if fc == 0 else [], a=[PBB[bk]] if fc else [])
                S.op("dve", lambda v: v.tensor_tensor(out=ysb[:, 512 * g:512 * g + 512], in0=PB[bk][:], in1=ysb[:, 512 * g:512 * g + 512], op=ALU.add), r=[PBB[bk], B.ysb], w=[B.ysb])
                yield
            S.dma("pool", "yout", ydst, ysb[:], r=[B.ysb])
            yield

        order = [("p", b, T) for b in range(NSEQ) for T in range(NT)]
        if SAMPLE:
            order.append(("s", 0, 0))
        N = len(order)

        def step(g, n):
            for _ in range(n):
                try:
                    next(g)
                except StopIteration:
                    return False
            return True

        load_tile(*order[0])
        load_cs(*order[0])
        for b in range(1):
            S.op("pool", lambda g: g.memset(Sst[:], 0.0), w=[B.Sst])
            S.op("pool", lambda g: g.memset(Sb[:], 0.0), w=[B.Sb])

        def mkA(i):
            kind, b, T = order[i]
            if kind == "p" and T == 0 and i > 0:
                S.op("pool", lambda g: g.memset(Sst[:], 0.0), w=[B.Sst])
                S.op("pool", lambda g: g.memset(Sb[:], 0.0), w=[B.Sb])
            return stageA(kind, b, T, i % 2, order[i + 1] if i + 1 < N else None)

        def drive(gB, gens):
            g0, g1, g2, g3 = gens if gens is not None else (None, None, None, None)
            live = {"b": gB is not None, "0": g0 is not None, "1": g1 is not None, "2": g2 is not None, "3": g3 is not None}
            n0 = 0
            while any(live.values()):
                if live["b"]:
                    live["b"] = step(gB, 1)
                if live["0"]:
                    k0 = A0HEAD if n0 < 6 else A0RATIO
                    live["0"] = step(g0, k0)
                    n0 += k0
                    if n0 >= A1START and live["1"]:
                        live["1"] = step(g1, 1)
                    if n0 >= A2START and live["2"]:
                        live["2"] = step(g2, 1)
                else:
                    if live["2"]:
                        live["2"] = step(g2, A2RATE)
                    if live["1"]:
                        live["1"] = step(g1, 1)
                    if live["3"]:
                        live["3"] = step(g3, 1)

        drive(None, mkA(0))
        for i in range(N):
            kind, b, T = order[i]
            drive(stageB(kind, b, T, i % 2), mkA(i + 1) if i + 1 < N else None)
        S.wait_all("sp")
    return nc


def core_inputs(c, x_prompt, x_sample, cache_k_win, cache_v_win, state_gla, norm_w, w_in, w_gate_up, b_gate,
                q_norm_w, k_norm_w, gla_norm_w, w_out, consts, NSEQ=2, NT=32):
    f = lambda a: np.ascontiguousarray(a, dtype=np.float32)
    m = dict(consts)
    m["xp"] = f(x_prompt[NSEQ * c:NSEQ * c + NSEQ, :NT * 128].reshape(NSEQ * NT * 128, 1024))
    m["xs"] = f(x_sample[16 * c:16 * c + 16].reshape(128, 1024))
    m["ck"] = f(cache_k_win[0, 16 * c:16 * c + 16].reshape(16, 2048, 512))
    m["cv"] = f(cache_v_win[0, 16 * c:16 * c + 16].reshape(16, 2048, 512))
    m["sg"] = f(state_gla[0, 16 * c:16 * c + 16])
    m["w_in"] = f(w_in[0])
    m["w_out"] = f(w_out[0])
    m["nw"] = f(norm_w[0].reshape(8, 128).T)
    m["qknw"] = f(np.broadcast_to(np.concatenate([q_norm_w[0], k_norm_w[0]])[None], (128, 128)))
    m["gnw"] = f(np.broadcast_to(gla_norm_w[0][None], (128, 128)))
    m["wgu"] = f(np.concatenate([w_gate_up[0], b_gate[0][None]], axis=0))
    return m


_CACHE = {}


def kernel(x_prompt, x_sample, cache_k_win, cache_v_win, state_gla, norm_w, w_in, w_gate_up, b_gate,
           q_norm_w, k_norm_w, gla_norm_w, w_out):
    args = [np.asarray(a) for a in (x_prompt, x_sample, cache_k_win, cache_v_win, state_gla, norm_w, w_in, w_gate_up,
                                    b_gate, q_norm_w, k_norm_w, gla_norm_w, w_out)]
    if "nc" not in _CACHE:
        _CACHE["nc"] = build()
        _CACHE["consts"] = make_consts(32)
    nc, consts = _CACHE["nc"], _CACHE["consts"]
    in_maps = [core_inputs(c, *args, consts) for c in range(8)]
    res = run_bass_kernel_spmd(nc, in_maps, core_ids=list(range(8))).results
    cat = lambda k: np.concatenate([r[k] for r in res], axis=0)
    y_p = cat("yp").reshape(16, 4096, 1024)
    y_s = cat("ys").reshape(128, 8, 1024)
    kwp = cat("kwp").reshape(1, 16, 2048, 8, 64)
    vwp = cat("vwp").reshape(1, 16, 2048, 8, 64)
    sgp = cat("sgp").reshape(1, 16, 4, 64, 128)
    kns = cat("kns").reshape(1, 128, 8, 8, 64)
    vns = cat("vns").reshape(1, 128, 8, 8, 64)
    sgs = cat("sgs").reshape(1, 128, 4, 64, 128)
    return (y_p, y_s, kwp, vwp, sgp, kns, vns, sgs)
```
